# Optimizing a Trainium2 kernel written in Bass

```python
import jax, jax.numpy as jnp
from jax import lax
import numpy as np

D_MODEL = 1024
BATCH = 2
SEQ = 8192
DEPTH = 1

POOL_WIDTH = D_MODEL // 2
POOL_WINDOWS = (2, 4, 8, 16)
POOL_GROUPS = len(POOL_WINDOWS)
POOL_GROUP_WIDTH = POOL_WIDTH // POOL_GROUPS
HEAD_DIM = 64
N_Q_HEADS = (D_MODEL // 2) // HEAD_DIM
N_KV_HEADS = 2
GQA_GROUP = N_Q_HEADS // N_KV_HEADS
ATTN_WIDTH = N_Q_HEADS * HEAD_DIM
KV_WIDTH = N_KV_HEADS * HEAD_DIM
WINDOW = 128
BLOCK = 128
ROPE_THETA = 10000.0
N_BRANCHES = 2
RMS_EPS = 1e-5
IN_WIDTHS = (POOL_WIDTH, POOL_WIDTH, ATTN_WIDTH, KV_WIDTH, KV_WIDTH, ATTN_WIDTH, N_BRANCHES * D_MODEL)
IN_COLS = int(sum(IN_WIDTHS))
SPLIT_POINTS = [int(v) for v in np.cumsum(IN_WIDTHS)[:-1]]

kernel_name = "hybrid_pool_swa_sink_gated_block"


def rms_norm(x, gain):
    xf = x.astype(jnp.float32)
    y = xf * lax.rsqrt(jnp.mean(xf * xf, axis=-1, keepdims=True) + RMS_EPS)
    return (y * gain.astype(jnp.float32)).astype(x.dtype)


def causal_multiscale_pool(u):
    B, S, _ = u.shape
    uf = u.astype(jnp.float32).reshape(B, S, POOL_GROUPS, POOL_GROUP_WIDTH)
    cs = jnp.cumsum(uf, axis=1)
    pos = jnp.arange(S)
    means = []
    for g, w in enumerate(POOL_WINDOWS):
        c = cs[:, :, g]
        prev = jnp.pad(c[:, :S - w], ((0, 0), (w, 0), (0, 0)))
        cnt = jnp.minimum(pos + 1, w).astype(jnp.float32)[None, :, None]
        means.append((c - prev) / cnt)
    mean = jnp.stack(means, axis=2)
    return (mean - uf).astype(u.dtype)


def rope_tables(S, dtype):
    pos = jnp.arange(S, dtype=jnp.float32)
    inv_freq = ROPE_THETA ** (-(jnp.arange(0, HEAD_DIM, 2, dtype=jnp.float32) / HEAD_DIM))
    ang = pos[:, None] * inv_freq[None, :]
    return jnp.cos(ang)[:, None, :].astype(dtype), jnp.sin(ang)[:, None, :].astype(dtype)


def apply_rope(t, cos, sin):
    t1, t2 = jnp.split(t, 2, axis=-1)
    return jnp.concatenate([t1 * cos - t2 * sin, t2 * cos + t1 * sin], axis=-1)


def sliding_window_gqa_with_sinks(q, k, v, sinks):
    B, S = q.shape[0], q.shape[1]
    nb = S // BLOCK
    qb = q.reshape(B, nb, BLOCK, N_KV_HEADS, GQA_GROUP, HEAD_DIM)

    def with_prev(t):
        tb = t.reshape(B, nb, BLOCK, N_KV_HEADS, HEAD_DIM)
        prev = jnp.pad(tb[:, :-1], ((0, 0), (1, 0), (0, 0), (0, 0), (0, 0)))
        return jnp.concatenate([prev, tb], axis=2)

    kb, vb = with_prev(k), with_prev(v)
    s = jnp.einsum('bnqkgd,bnskd->bnkgqs', qb, kb,
                   preferred_element_type=jnp.float32) * (HEAD_DIM ** -0.5)
    blk = jnp.arange(nb)[:, None, None] * BLOCK
    qpos = blk + jnp.arange(BLOCK)[None, :, None]
    kpos = blk - BLOCK + jnp.arange(2 * BLOCK)[None, None, :]
    delta = qpos - kpos
    mask = (delta >= 0) & (delta < WINDOW) & (kpos >= 0)
    s = jnp.where(mask[None, :, None, None], s, jnp.float32(-1e30))
    sink = jnp.broadcast_to(sinks.astype(jnp.float32).reshape(1, 1, N_KV_HEADS, GQA_GROUP, 1, 1),
                            s.shape[:-1] + (1,))
    p = jax.nn.softmax(jnp.concatenate([s, sink], axis=-1), axis=-1)[..., :-1]
    o = jnp.einsum('bnkgqs,bnskd->bnqkgd', p.astype(v.dtype), vb)
    return o.reshape(B, S, ATTN_WIDTH)


def setup_inputs(seed: int = 0) -> dict:
    key = jax.random.key(seed)
    ks = jax.random.split(key, 11)
    f32 = jnp.float32
    x = jax.random.normal(ks[0], (BATCH, SEQ, D_MODEL), f32)
    norm_gain = 1.0 + 0.02 * jax.random.normal(ks[1], (DEPTH, D_MODEL), f32)
    w_in = jax.random.normal(ks[2], (DEPTH, D_MODEL, IN_COLS), f32) * D_MODEL ** -0.5
    pool_w = jax.random.normal(ks[3], (DEPTH, POOL_GROUPS, POOL_GROUP_WIDTH, POOL_GROUP_WIDTH), f32) * POOL_GROUP_WIDTH ** -0.5
    pool_scale = 1.0 + 0.1 * jax.random.normal(ks[4], (DEPTH, POOL_WIDTH), f32)
    attn_sinks = 0.5 * jax.random.normal(ks[5], (DEPTH, N_Q_HEADS), f32)
    w_branch_pool = jax.random.normal(ks[6], (DEPTH, POOL_WIDTH, D_MODEL), f32) * POOL_WIDTH ** -0.5
    w_branch_attn = jax.random.normal(ks[7], (DEPTH, ATTN_WIDTH, D_MODEL), f32) * ATTN_WIDTH ** -0.5
    w_out = jax.random.normal(ks[8], (DEPTH, D_MODEL, D_MODEL), f32) * D_MODEL ** -0.5
    final_gain = 1.0 + 0.02 * jax.random.normal(ks[9], (D_MODEL,), f32)
    return {"x": x, "norm_gain": norm_gain, "w_in": w_in, "pool_w": pool_w,
            "pool_scale": pool_scale, "attn_sinks": attn_sinks,
            "w_branch_pool": w_branch_pool, "w_branch_attn": w_branch_attn,
            "w_out": w_out, "final_gain": final_gain}


def reference(x, norm_gain, w_in, pool_w, pool_scale, attn_sinks,
              w_branch_pool, w_branch_attn, w_out, final_gain):
    B, S, _ = x.shape
    cos, sin = rope_tables(S, x.dtype)
    for l in range(DEPTH):
        h = rms_norm(x, norm_gain[l])
        proj = h @ w_in[l]
        pool_u, pool_z, q, k, v, attn_z, gate_logits = jnp.split(proj, SPLIT_POINTS, axis=-1)
        pooled = causal_multiscale_pool(pool_u)
        pooled = jnp.einsum('bsgc,gcd->bsgd', pooled, pool_w[l]).reshape(B, S, POOL_WIDTH) * pool_scale[l]
        pool_branch = (pooled * jax.nn.silu(pool_z)) @ w_branch_pool[l]
        q = apply_rope(q.reshape(B, S, N_Q_HEADS, HEAD_DIM), cos, sin)
        k = apply_rope(k.reshape(B, S, N_KV_HEADS, HEAD_DIM), cos, sin)
        v = v.reshape(B, S, N_KV_HEADS, HEAD_DIM)
        o = sliding_window_gqa_with_sinks(q, k, v, attn_sinks[l])
        attn_branch = (o * jax.nn.silu(attn_z)) @ w_branch_attn[l]
        gates = jax.nn.sigmoid(gate_logits).reshape(B, S, N_BRANCHES, D_MODEL)
        merged = gates[:, :, 0] * pool_branch + gates[:, :, 1] * attn_branch
        x = x + merged @ w_out[l]
    return rms_norm(x, final_gain)
```

```python
import contextlib
import sys
import numpy as np
import concourse.bass as bass
import concourse.mybir as mybir
from concourse.bass_utils import run_bass_kernel_spmd

F32 = mybir.dt.float32
BF16 = mybir.dt.bfloat16
AF = mybir.ActivationFunctionType
ALU = mybir.AluOpType

NCORES = 8
WSTEP = 1024
D = 1024
SEQ = 8192
T = 2048
W = 512
NS = T // W
IN_COLS = 4352
RMS_EPS = 1e-5
C_K, C_V, C_PU, C_Q, C_Z, C_PZ, C_G0, C_G1 = 0, 128, 256, 768, 1280, 1792, 2304, 3328
NDS = {"sp": 24, "pool": 20}
CS_GAIN, CS_PSC, CS_SINK, CS_INVC, CS_NEGH, NCST = 0, 8, 12, 20, 84, 85


_LAST = {}


class Buf:
    __slots__ = ("name", "w", "r")

    def __init__(self, name):
        self.name = name
        self.w = None
        self.r = {}


class Sched:
    def __init__(self):
        self.streams = {e: [] for e in ("pe", "act", "dve", "pool", "sp")}
        self.cnt = {}
        self.seen = {e: {} for e in self.streams}
        self.labels = {e: [] for e in self.streams}
        self.ndma = {}

    def op(self, eng, fn, reads=(), writes=(), dma=False):
        waits = {}
        fr = sys._getframe(1)
        label = f"{fr.f_code.co_name}:{fr.f_lineno}"

        def need(tok):
            if tok is None:
                return
            k, v = tok
            if v > waits.get(k, 0):
                waits[k] = v

        for b in reads:
            need(b.w)
        for b in writes:
            need(b.w)
            for k, v in b.r.items():
                need((k, v))
        if fn is None:
            key, inc, tok = None, 0, None
        else:
            if dma:
                n = self.ndma.get(eng, 0)
                self.ndma[eng] = n + 1
                key = ("d" + eng, n % NDS[eng])
                prev = self.cnt.get(key, 0)
                if prev:
                    need((key, prev))
                inc = 16
            else:
                key, inc = eng, 1
            val = self.cnt.get(key, 0) + inc
            self.cnt[key] = val
            tok = (key, val)
        seen = self.seen[eng]
        wl = []
        for k, v in waits.items():
            if k == "pe" and eng == "pe":
                continue
            if seen.get(k, 0) >= v:
                continue
            seen[k] = v
            wl.append((k, v))
        self.streams[eng].append((fn, wl, key, inc))
        self.labels[eng].append((label, tok, wl))
        if tok is not None:
            for b in reads:
                if b.r.get(tok[0], 0) < tok[1]:
                    b.r[tok[0]] = tok[1]
            for b in writes:
                b.w = tok
                b.r = {}
        return tok


def build_program(debug_taps=()):
    nc = bass.Bass("TRN2", target_bir_lowering=False)

    def din(name, shape):
        return nc.dram_tensor(name, list(shape), F32, kind="ExternalInput").ap()

    x_c = din("x_c", [T, D])
    x_h = din("x_h", [128, D])
    w_in_d = din("w_in_l", [128, 8, IN_COLS])
    w_bp_d = din("w_bp_l", [128, 4, D])
    w_ba_d = din("w_ba_l", [128, 4, D])
    w_out_d = din("w_out_l", [128, 8, D])
    pw_d = din("pool_w_l", [128, 4, 128])
    cst_d = din("cst", [128, NCST])
    fg_d = din("fg_b", [128, D])
    cos_d = din("cosT", [128, 128 + T])
    sin_d = din("sinT", [128, 128 + T])
    msk_d = din("masks", [128, 512])
    idr_d = din("idr", [128, 256])
    out_c = nc.dram_tensor("out_c", [T, D], F32, kind="ExternalOutput").ap()
    taps = {}
    for name, shape, tdt in debug_taps:
        taps[name] = nc.dram_tensor("tap_" + name, list(shape), tdt, kind="ExternalOutput").ap()

    S = Sched()
    _LAST['S'] = S
    es = contextlib.ExitStack()

    def sb(name, shape, dt):
        return es.enter_context(nc.sbuf_tensor("sb_" + name, list(shape), dt))

    with es:
        w_in = sb("w_in", [128, 8, IN_COLS], BF16)
        w_bp = sb("w_bp", [128, 4, D], BF16)
        w_ba = sb("w_ba", [128, 4, D], BF16)
        w_out = sb("w_out", [128, 8, D], BF16)
        pw = sb("pw", [128, 4, 128], BF16)
        cst = sb("cst", [128, NCST], F32)
        fg = sb("fg", [128, D], F32)
        msk = sb("msk", [128, 512], BF16)
        idb = sb("idb", [128, 128], BF16)
        rm = sb("rm", [128, 128], F32)
        esink = sb("esink", [128, 8], F32)
        psc = sb("psc", [128, 4], F32)
        xa = [sb(f"xa{i}", [128, D], F32) for i in range(2)]
        xs = [sb(f"xs{i}", [128, D], BF16) for i in range(2)]
        hT = [sb(f"hT{i}", [128, 8, W], BF16) for i in range(2)]
        qkf = sb("qkf", [128, W], F32)
        t2 = sb("t2", [128, W], F32)
        cs = sb("cs", [128, W], F32)
        sn = sb("sn", [128, W], F32)
        csh = sb("csh", [128, 128], F32)
        snh = sb("snh", [128, 128], F32)
        qT = sb("qT", [128, 4, W], BF16)
        kT = sb("kT", [128, 9 * 128], BF16)
        vA = sb("vA", [128, 9, 2, 65], BF16)
        sz = [sb("sz0", [128, 512], F32)]
        PT = [sb(f"PT{i}", [128, 2, 4, 128], BF16) for i in range(2)]
        go = [sb(f"go{i}", [128, 512], BF16) for i in range(2)]
        goT = [sb(f"goT{i}", [128, 4, W], BF16) for i in range(2)]
        pu = sb("pu", [128, 16 + W], F32)
        pA = sb("pA", [128, 16 + W], F32)
        pB = sb("pB", [128, 16 + W], F32)
        hist = sb("hist", [128, 4, 16], F32)
        pooled = [sb(f"pooled{i}", [128, W], BF16) for i in range(2)]
        spz = sb("spz", [128, W], F32)
        gpT = [sb(f"gpT{i}", [128, 4, W], BF16) for i in range(2)]
        sg0 = sb("sg0", [128, W], F32)
        sg1 = sb("sg1", [128, W], F32)
        mT = sb("mT", [128, 8, W], BF16)
        ro = [sb(f"ro{i}", [128, D], F32) for i in range(2)]
        st = sb("st", [128, 16], F32)
        den = sb("den", [128, 8], F32)
        rden = sb("rden", [128, 8], F32)
        ps = es.enter_context(nc.psum_tensor("ps", [128, 8 * 512], F32))

        B = Buf
        b_w = {k: B("w_" + k) for k in ("k", "v", "pu", "q", "z", "pz", "g0", "g1", "bp", "ba", "out", "pw")}
        b_cst, b_fg, b_msk, b_idb, b_rm, b_esink, b_psc = (B(n) for n in ("cst", "fg", "msk", "idb", "rm", "esink", "psc"))
        b_xa = [B("xa0"), B("xa1")]
        b_qkf, b_t2, b_cs, b_sn = (B(n) for n in ("qkf", "t2", "cs", "sn"))
        b_xs = [B("xs0"), B("xs1")]
        b_hT = [[B(f"hT{i}_{t}") for t in range(4)] for i in range(2)]
        b_qT = [B(f"qT{c}") for c in range(4)]
        b_kT = [B(f"kT{i}") for i in range(9)]
        b_vA = [B(f"vA{i}") for i in range(9)]
        b_vones = B("vones")
        b_sz = [B("sz0")]
        b_PT = [B("PT0"), B("PT1")]
        b_go = [B("go0"), B("go1")]
        b_goT = [[B(f"goT{i}_{t}") for t in range(4)] for i in range(2)]
        b_pu, b_pA, b_pB, b_spz = B("pu"), B("pA"), B("pB"), B("spz")
        b_hist = [B(f"hist{g}") for g in range(4)]
        b_pooled = [B("pooled0"), B("pooled1")]
        b_gpT = [[B(f"gpT{i}_{g}") for g in range(4)] for i in range(2)]
        b_sg0, b_sg1 = B("sg0"), B("sg1")
        b_mT = [B(f"mT{i}") for i in range(8)]
        b_mTh = [[B(f"mT{i}_h{h}") for i in range(8)] for h in range(2)]
        b_ro = [B("ro0"), B("ro1")]
        b_st = [B(f"st{i}") for i in range(16)]
        b_den, b_rden = B("den"), B("rden")
        b_bank = [B(f"bank{i}") for i in range(8)]
        b_out = []

        free_banks = list(range(8))

        def alloc1():
            for b_ in free_banks:
                if b_ >= 4:
                    free_banks.remove(b_)
                    return b_
            return free_banks.pop(0)

        def alloc2():
            best = None
            for p in range(0, 8, 2):
                if p in free_banks and (p + 1) in free_banks:
                    age = max(free_banks.index(p), free_banks.index(p + 1)) + (100 if p >= 4 else 0)
                    if best is None or age < best[0]:
                        best = (age, p)
            if best is None:
                raise RuntimeError("no free PSUM bank pair")
            p = best[1]
            free_banks.remove(p)
            free_banks.remove(p + 1)
            return p

        def rel(*bks):
            for b_ in bks:
                assert b_ not in free_banks
                free_banks.append(b_)

        def bank(i, n=512):
            return ps[:, i * 512:i * 512 + n]

        def bank_bf(i):
            return ps[:, i * 512:(i + 1) * 512].bitcast(BF16)

        op = S.op

        def tap(name, ap, rd):
            if name in taps:
                bo = B("tap_" + name)
                b_out.append(bo)
                op("sp", lambda e: e.dma_start(out=taps[name], in_=ap), reads=rd, writes=[bo], dma=True)

        def load(dst, src, wr, eng="sp"):
            op(eng, lambda e: e.dma_start(out=dst, in_=src), writes=wr, dma=True)

        def wload(dst, src, wr):
            op("pool", lambda e: e.dma_start(out=dst, in_=src, max_dma_last_dim=4096), writes=wr, dma=True)

        def wcols(keys, c0, n, step):
            for a_ in range(c0, c0 + n, step):
                m = min(step, c0 + n - a_)
                wload(w_in[:, :, a_:a_ + m], w_in_d[:, :, a_:a_ + m], [b_w[k_] for k_ in keys])

        load(xa[0][:, :], x_h[:, :], [b_xa[0]])
        load(cst[:, :], cst_d[:, :], [b_cst])
        load(rm[:, :], idr_d[:, 128:256], [b_rm])
        load(qkf[:, 0:128], idr_d[:, 0:128], [b_qkf])
        load(t2[:, 0:512], msk_d[:, :], [b_t2])
        b_csh, b_snh = B("csh"), B("snh")
        load(csh[:, :], cos_d[:, 0:128], [b_csh])
        load(snh[:, :], sin_d[:, 0:128], [b_snh])
        op("dve", lambda e: e.tensor_copy(out=idb[:, :], in_=qkf[:, 0:128]), reads=[b_qkf], writes=[b_idb])
        op("dve", lambda e: e.tensor_copy(out=msk[:, :], in_=t2[:, 0:512]), reads=[b_t2], writes=[b_msk])
        op("act", lambda e: e.activation(out=esink[:, :], in_=cst[:, CS_SINK:CS_SINK + 8], func=AF.Exp),
           reads=[b_cst], writes=[b_esink])
        op("dve", lambda e: e.tensor_scalar(out=psc[:, :], in0=cst[:, CS_PSC:CS_PSC + 4], scalar1=0.5, scalar2=None,
                                            op0=ALU.mult), reads=[b_cst], writes=[b_psc])
        op("pool", lambda e: e.memset(vA[:, :, :, :], 1.0), writes=[b_vones])
        wcols(["k", "v"], C_K, 256, 256)
        wcols(["q"], C_Q, 512, 512)
        wcols(["pu"], C_PU, 512, 512)
        wcols(["z"], C_Z, 512, 512)

        def weights_mid():
            wload(pw[:, :, :], pw_d[:, :, :], [b_w["pw"]])
            wcols(["pz"], C_PZ, 512, 512)

        def weights_late():
            wload(w_bp[:, :, :], w_bp_d[:, :, :], [b_w["bp"]])
            wload(w_ba[:, :, :], w_ba_d[:, :, :], [b_w["ba"]])
            wcols(["g0"], C_G0, 1024, WSTEP)
            wcols(["g1"], C_G1, 1024, WSTEP)
            wload(w_out[:, :, :], w_out_d[:, :, :], [b_w["out"]])
            load(fg[:, :], fg_d[:, :], [b_fg])

        stc = {"i": 0}
        cnt = {"rope": 0}

        def st_slot():
            i = stc["i"] % 16
            stc["i"] += 1
            return i

        def rstd_of(src_ap, src_bufs, junk_ap, junk_bufs):
            i = st_slot()
            j = st_slot()
            op("act", lambda e: e.activation(out=junk_ap, in_=src_ap, func=AF.Square, accum_out=st[:, i:i + 1]),
               reads=src_bufs, writes=junk_bufs + [b_st[i]])
            op("pool", lambda e: e.tensor_scalar(out=st[:, j:j + 1], in0=st[:, i:i + 1], scalar1=1.0 / D,
                                                 scalar2=RMS_EPS, op0=ALU.mult, op1=ALU.add),
               reads=[b_st[i]], writes=[b_st[j]])
            op("pool", lambda e: e.tensor_tensor(out=st[:, i:i + 1], in0=st[:, j:j + 1],
                                                 in1=cst[:, CS_NEGH:CS_NEGH + 1], op=ALU.pow),
               reads=[b_st[j], b_cst], writes=[b_st[i]])
            return i

        gain_b = cst[:, CS_GAIN:CS_GAIN + 8].unsqueeze(2).to_broadcast([128, 8, 128])

        def xbuf(xbuf_i):
            return (xa[xbuf_i], b_xa[xbuf_i]) if xbuf_i < 2 else (ro[xbuf_i - 2], b_ro[xbuf_i - 2])

        def norm_sq(xbuf_i, xsi, junk=None):
            xt, bx = xbuf(xbuf_i)
            if junk is None:
                return rstd_of(xt[:, :], [bx], xs[xsi][:, :], [b_xs[xsi]])
            return rstd_of(xt[:, :], [bx], PT[junk][:, :, :, :].rearrange("p a j q -> p (a j q)"), [b_PT[junk]])

        def norm_cp(xbuf_i, xsi, i):
            xt, bx = xbuf(xbuf_i)
            op("act", lambda e: e.activation(out=xs[xsi][:, :], in_=xt[:, :], func=AF.Copy, scale=st[:, i:i + 1]),
               reads=[bx, b_st[i]], writes=[b_xs[xsi]])

        def norm_pre(xbuf_i, xsi):
            norm_cp(xbuf_i, xsi, norm_sq(xbuf_i, xsi))

        def norm_pe(xsi, hi, t):
            bk = alloc1()
            tp = bank_bf(bk)
            for kc in range(8):
                op("pe", lambda e, kc=kc: e.transpose(out=tp[:, kc * 128:(kc + 1) * 128],
                                                      in_=xs[xsi][:, kc * 128:(kc + 1) * 128], identity=idb[:, :]),
                   reads=[b_xs[xsi], b_idb], writes=[b_bank[bk]])
            op("dve", lambda e: e.tensor_tensor(out=hT[hi][:, :, t * 128:(t + 1) * 128],
                                                in0=tp.rearrange("p (k t) -> p k t", k=8), in1=gain_b, op=ALU.mult),
               reads=[b_bank[bk], b_cst], writes=[b_hT[hi][t]])
            rel(bk)

        def norm_tile(xbuf_i, hi, t):
            norm_pre(xbuf_i, 0)
            norm_pe(0, hi, t)

        def fm_chunk(c0, wkey, hi, n):
            bk = alloc1()
            for kc in range(8):
                op("pe", lambda e, kc=kc: e.matmul(bank(bk, n), lhsT=w_in[:, kc, c0:c0 + 128], rhs=hT[hi][:, kc, 0:n],
                                                   start=(kc == 0), stop=(kc == 7)),
                   reads=[b_w[wkey]] + b_hT[hi][0:max(1, n // 128)], writes=[b_bank[bk]])
            return bk

        rope_ring = [(qkf, t2, b_qkf, b_t2), (pA, pB, b_pA, b_pB)]

        def rope_copy(bk, n, slot):
            qf, tf, bqf, btf = rope_ring[slot]
            op("act", lambda e: e.activation(out=qf[:, 0:n], in_=bank(bk, n), func=AF.Copy),
               reads=[b_bank[bk]], writes=[bqf])
            rel(bk)

        def rope_rest(n, dst_ap, dst_bufs, tabs=None, slot=0):
            cs_, sn_, bcs_, bsn_ = tabs if tabs is not None else (cs, sn, b_cs, b_sn)
            qf, tf, bqf, btf = rope_ring[slot]
            b2 = alloc1()
            op("pe", lambda e: e.matmul(bank(b2, n), lhsT=rm[:, :], rhs=qf[:, 0:n], start=True, stop=True),
               reads=[b_rm, bqf], writes=[b_bank[b2]])
            op("dve", lambda e: e.tensor_tensor(out=tf[:, 0:n], in0=bank(b2, n), in1=sn_[:, 0:n], op=ALU.mult),
               reads=[b_bank[b2], bsn_], writes=[btf])
            rel(b2)
            op("dve", lambda e: e.tensor_tensor(out=qf[:, 0:n], in0=qf[:, 0:n], in1=cs_[:, 0:n], op=ALU.mult),
               reads=[bqf, bcs_], writes=[bqf])
            op("dve", lambda e: e.tensor_tensor(out=dst_ap, in0=qf[:, 0:n], in1=tf[:, 0:n], op=ALU.add),
               reads=[bqf, btf], writes=dst_bufs)

        def rope_chunk(bk, n, dst_ap, dst_bufs, tabs=None, slot=0):
            rope_copy(bk, n, slot)
            rope_rest(n, dst_ap, dst_bufs, tabs, slot)

        def v_part(gt, hi, t):
            sl = gt % 9
            bv = alloc1()
            for kc in range(8):
                op("pe", lambda e, kc=kc: e.matmul(bank(bv, 128), lhsT=hT[hi][:, kc, t * 128:(t + 1) * 128],
                                                   rhs=w_in[:, kc, C_V:C_V + 128], start=(kc == 0), stop=(kc == 7)),
                   reads=[b_w["v"], b_hT[hi][t]], writes=[b_bank[bv]])
            op("act", lambda e: e.activation(out=vA[:, sl, :, 0:64],
                                             in_=bank(bv, 128).rearrange("p (g d) -> p g d", g=2), func=AF.Copy),
               reads=[b_bank[bv], b_vones], writes=[b_vA[sl]])
            rel(bv)

        def z_part(hi, t, zi):
            bz = alloc1()
            for kc in range(8):
                op("pe", lambda e, kc=kc: e.matmul(bank(bz), lhsT=hT[hi][:, kc, t * 128:(t + 1) * 128],
                                                   rhs=w_in[:, kc, C_Z:C_Z + 512], start=(kc == 0), stop=(kc == 7)),
                   reads=[b_w["z"], b_hT[hi][t]], writes=[b_bank[bz]])
            op("act", lambda e: e.activation(out=sz[zi][:, :], in_=bank(bz), func=AF.Tanh, scale=0.5),
               reads=[b_bank[bz]], writes=[b_sz[zi]])
            op("dve", lambda e: e.scalar_tensor_tensor(out=sz[zi][:, :], in0=sz[zi][:, :], scalar=1.0,
                                                       in1=bank(bz), op0=ALU.add, op1=ALU.mult),
               reads=[b_sz[zi], b_bank[bz]], writes=[b_sz[zi]])
            rel(bz)

        norm_tile(0, 1, 0)

        def halo_proj():
            bk = fm_chunk(C_K, "k", 1, 128)
            rope_chunk(bk, 128, kT[:, 0:128], [b_kT[0]], (csh, snh, b_csh, b_snh), slot=1)
            v_part(0, 1, 0)

        def halo_pu():
            for g in range(4):
                bk = fm_chunk(C_PU + g * 128, "pu", 1, 128)
                op("dve", lambda e, bk=bk, g=g: e.tensor_copy(out=hist[:, g, :], in_=bank(bk, 128)[:, 112:128]),
                   reads=[b_bank[bk]], writes=[b_hist[g]])
                rel(bk)

        def x_rows(s, t):
            r0 = s * W + t * 128
            return x_c[r0:r0 + 128, :]

        cnt.update({"att": 0, "fin": 0, "ro": 0, "xa": 1, "z": 0})

        def att_scores(s, t, g, pti):
            gt = 1 + 4 * s + t
            sb2 = alloc2()
            S2 = ps[:, sb2 * 512:(sb2 + 2) * 512]
            rq = qT[64 * g:64 * g + 64, :, t * 128:(t + 1) * 128]
            for kb in range(2):
                sl = (gt - 1 + kb) % 9
                op("pe", lambda e, kb=kb, sl=sl: e.matmul(
                    bank(sb2 + kb).rearrange("p (j q) -> p j q", j=4),
                    lhsT=kT[64 * g:64 * g + 64, sl * 128:(sl + 1) * 128], rhs=rq, start=True, stop=True),
                   reads=[b_kT[sl]] + b_qT, writes=[b_bank[sb2 + kb]])
            op("act", lambda e: e.activation(out=PT[pti][:, :, :, :].rearrange("p a j q -> p (a j q)"), in_=S2,
                                             func=AF.Exp, scale=0.125),
               reads=[b_bank[sb2], b_bank[sb2 + 1]], writes=[b_PT[pti]])
            rel(sb2, sb2 + 1)
            m0 = 256 if (s == 0 and t == 0) else 0
            mpair = msk[:, m0:m0 + 256].rearrange("p (a q) -> p a q", a=2).unsqueeze(2).to_broadcast([128, 2, 4, 128])
            op("dve", lambda e: e.tensor_tensor(out=PT[pti][:, :, :, :], in0=PT[pti][:, :, :, :], in1=mpair,
                                                op=ALU.mult),
               reads=[b_PT[pti], b_msk], writes=[b_PT[pti]])

        def att_scores_both(s, t, p0):
            gt = 1 + 4 * s + t
            sbs = [alloc2(), alloc2()]
            for kb in range(2):
                sl = (gt - 1 + kb) % 9
                for g in range(2):
                    rq = qT[64 * g:64 * g + 64, :, t * 128:(t + 1) * 128]
                    op("pe", lambda e, kb=kb, sl=sl, g=g, rq=rq: e.matmul(
                        bank(sbs[g] + kb).rearrange("p (j q) -> p j q", j=4),
                        lhsT=kT[64 * g:64 * g + 64, sl * 128:(sl + 1) * 128], rhs=rq, start=True, stop=True),
                       reads=[b_kT[sl]] + b_qT, writes=[b_bank[sbs[g] + kb]])
            m0 = 256 if (s == 0 and t == 0) else 0
            mpair = msk[:, m0:m0 + 256].rearrange("p (a q) -> p a q", a=2).unsqueeze(2).to_broadcast([128, 2, 4, 128])
            for g in range(2):
                pti = p0 if g == 0 else 1 - p0
                sb2 = sbs[g]
                S2 = ps[:, sb2 * 512:(sb2 + 2) * 512]
                op("act", lambda e, pti=pti, S2=S2: e.activation(
                    out=PT[pti][:, :, :, :].rearrange("p a j q -> p (a j q)"), in_=S2, func=AF.Exp, scale=0.125),
                   reads=[b_bank[sb2], b_bank[sb2 + 1]], writes=[b_PT[pti]])
                rel(sb2, sb2 + 1)
                op("dve", lambda e, pti=pti: e.tensor_tensor(out=PT[pti][:, :, :, :], in0=PT[pti][:, :, :, :], in1=mpair,
                                                            op=ALU.mult),
                   reads=[b_PT[pti], b_msk], writes=[b_PT[pti]])

        def att_pv(s, t, g, pti, oa):
            gt = 1 + 4 * s + t
            for j in range(4):
                oap = ps[:, (oa + g) * 512 + j * 65:(oa + g) * 512 + j * 65 + 65]
                for kb in range(2):
                    sl = (gt - 1 + kb) % 9
                    op("pe", lambda e, j=j, kb=kb, oap=oap, sl=sl: e.matmul(
                        oap, lhsT=PT[pti][:, kb, j, :], rhs=vA[:, sl, g, :], start=(kb == 0), stop=(kb == 1)),
                       reads=[b_PT[pti], b_vA[sl], b_vones], writes=[b_bank[oa + g]])

        def att_finish_dve(oa, zi, gi):
            O = ps[:, oa * 512:(oa + 2) * 512].rearrange("p (g c) -> p g c", g=2)[:, :, 0:260] \
                .rearrange("p g (j e) -> p g j e", e=65)
            op("dve", lambda e: e.tensor_tensor(out=den[:, :].rearrange("p (g j) -> p g j", g=2), in0=O[:, :, :, 64],
                                                in1=esink[:, :].rearrange("p (g j) -> p g j", g=2), op=ALU.add),
               reads=[b_bank[oa], b_bank[oa + 1], b_esink], writes=[b_den])
            op("dve", lambda e: e.reciprocal(out=rden[:, :], in_=den[:, :]), reads=[b_den], writes=[b_rden])
            rb = rden[:, :].rearrange("p (g j) -> p g j", g=2).unsqueeze(3).to_broadcast([128, 2, 4, 64])
            op("dve", lambda e: e.tensor_tensor(out=sz[zi][:, :].rearrange("p (g j d) -> p g j d", g=2, j=4),
                                                in0=sz[zi][:, :].rearrange("p (g j d) -> p g j d", g=2, j=4),
                                                in1=rb, op=ALU.mult),
               reads=[b_sz[zi], b_rden], writes=[b_sz[zi]])
            op("dve", lambda e: e.tensor_tensor(out=go[gi][:, :].rearrange("p (g j d) -> p g j d", g=2, j=4),
                                                in0=O[:, :, :, 0:64],
                                                in1=sz[zi][:, :].rearrange("p (g j d) -> p g j d", g=2, j=4),
                                                op=ALU.mult),
               reads=[b_bank[oa], b_bank[oa + 1], b_sz[zi]], writes=[b_go[gi]])
            rel(oa, oa + 1)

        def att_finish_pe(gi_buf, t, gi):
            bk = alloc1()
            tp = bank_bf(bk)
            for c in range(4):
                op("pe", lambda e, c=c: e.transpose(out=tp[:, c * 128:(c + 1) * 128],
                                                    in_=go[gi][:, c * 128:(c + 1) * 128], identity=idb[:, :]),
                   reads=[b_go[gi], b_idb], writes=[b_bank[bk]])
            op("act", lambda e: e.activation(out=goT[gi_buf][:, :, t * 128:(t + 1) * 128],
                                             in_=tp[:, 0:512].rearrange("p (c q) -> p c q", c=4), func=AF.Copy, scale=0.5),
               reads=[b_bank[bk]], writes=[b_goT[gi_buf][t]])
            rel(bk)

        def pool_a(s, g, hi):
            w = 2 << g
            pi = g % 2
            bk = fm_chunk(C_PU + g * 128, "pu", hi, W)
            op("act", lambda e: e.activation(out=pu[:, 16:16 + W], in_=bank(bk), func=AF.Copy),
               reads=[b_bank[bk]], writes=[b_pu])
            rel(bk)
            op("pool", lambda e: e.tensor_copy(out=pu[:, 0:16], in_=hist[:, g, :]), reads=[b_hist[g]], writes=[b_pu])
            op("pool", lambda e: e.tensor_copy(out=hist[:, g, :], in_=pu[:, W:W + 16]), reads=[b_pu], writes=[b_hist[g]])
            U = pu
            N = 16 + W
            op("pool", lambda e: e.tensor_tensor(out=pA[:, 1:N], in0=U[:, 1:N], in1=U[:, 0:N - 1], op=ALU.add),
               reads=[b_pu], writes=[b_pA])
            cur, curb, oth, othb = pA, b_pA, pB, b_pB
            sh, lo = 2, 1
            while sh < w:
                lo2 = lo + sh
                op("pool", lambda e, cur=cur, oth=oth, sh=sh, lo2=lo2: e.tensor_tensor(
                    out=oth[:, lo2:N], in0=cur[:, lo2:N], in1=cur[:, lo2 - sh:N - sh], op=ALU.add),
                   reads=[curb], writes=[othb])
                cur, curb, oth, othb = oth, othb, cur, curb
                sh, lo = sh * 2, lo2
            return lambda: pool_a_fin(s, g, pi, w, N, U, cur, curb, oth, othb)

        def pool_a_fin(s, g, pi, w, N, U, cur, curb, oth, othb):
            op("dve", lambda e, cur=cur: e.scalar_tensor_tensor(out=pooled[pi][:, :], in0=cur[:, 16:N], scalar=1.0 / w,
                                                                in1=U[:, 16:N], op0=ALU.mult, op1=ALU.subtract),
               reads=[curb, b_pu], writes=[b_pooled[pi]])
            if s == 0:
                ic = cst[:, CS_INVC + g * 16:CS_INVC + (g + 1) * 16]
                op("dve", lambda e, cur=cur, oth=oth: e.tensor_tensor(out=oth[:, 0:16], in0=cur[:, 16:32], in1=ic,
                                                                      op=ALU.mult),
                   reads=[curb, b_cst], writes=[othb])
                op("dve", lambda e, oth=oth: e.tensor_tensor(out=pooled[pi][:, 0:16], in0=oth[:, 0:16], in1=U[:, 16:32],
                                                             op=ALU.subtract),
                   reads=[othb, b_pu], writes=[b_pooled[pi]])

        def pool_b(g, hi, gb):
            pi = g % 2
            bz = fm_chunk(C_PZ + g * 128, "pz", hi, W)
            op("act", lambda e: e.activation(out=spz[:, :], in_=bank(bz), func=AF.Tanh, scale=0.5),
               reads=[b_bank[bz]], writes=[b_spz])
            op("dve", lambda e: e.scalar_tensor_tensor(out=spz[:, :], in0=spz[:, :], scalar=1.0, in1=bank(bz),
                                                       op0=ALU.add, op1=ALU.mult),
               reads=[b_spz, b_bank[bz]], writes=[b_spz])
            rel(bz)
            bm = alloc1()
            op("pe", lambda e: e.matmul(bank(bm), lhsT=pw[:, g, :], rhs=pooled[pi][:, :], start=True, stop=True),
               reads=[b_w["pw"], b_pooled[pi]], writes=[b_bank[bm]])
            op("dve", lambda e: e.scalar_tensor_tensor(out=gpT[gb][:, g, :], in0=bank(bm), scalar=psc[:, g:g + 1],
                                                       in1=spz[:, :], op0=ALU.mult, op1=ALU.mult),
               reads=[b_bank[bm], b_psc, b_spz], writes=[b_gpT[gb][g]])
            rel(bm)

        def stage_a1(s):
            hi = s % 2
            gt0 = 1 + 4 * s

            head = s <= 1

            def sq(t):
                if head:
                    xi = (1, 2, 3, 0)[t]
                    xt, bx = xbuf(xi)
                    load(xt[:, :], x_rows(s, t), [bx])
                    return (xi, t % 2, norm_sq(xi, t % 2, junk=(t - 2 if t >= 2 else None)))
                xi = cnt["xa"] % 2
                cnt["xa"] += 1
                load(xa[xi][:, :], x_rows(s, t), [b_xa[xi]])
                return (xi, t % 2, norm_sq(xi, t % 2))

            p0_, p1_ = sq(0), sq(1)
            norm_cp(*p0_)
            norm_cp(*p1_)
            pend = [sq(2), sq(3)] if head else None
            yield
            load(cs[:, :], cos_d[:, 128 + s * W:128 + (s + 1) * W], [b_cs])
            load(sn[:, :], sin_d[:, 128 + s * W:128 + (s + 1) * W], [b_sn])
            for t in range(4):
                norm_pe(t % 2, hi, t)
                pn = None
                if t + 2 < 4:
                    pn = pend[t] if head else sq(t + 2)
                if s > 0 and 0 < t < 3:
                    v_part(gt0 + t - 1, hi, t - 1)
                if pn is not None:
                    norm_cp(*pn)
                yield
            if s == 0:
                halo_proj()
                for t in range(4):
                    v_part(gt0 + t, hi, t)
                yield
            sl0 = gt0 % 9
            if s > 0:
                v_part(gt0 + 2, hi, 2)
                yield
            bk = fm_chunk(C_K, "k", hi, W)
            rope_copy(bk, W, 0)
            if s > 0:
                v_part(gt0 + 3, hi, 3)
            yield
            rope_rest(W, kT[:, sl0 * 128:sl0 * 128 + W], [b_kT[sl0 + i] for i in range(4)], slot=0)
            yield

        def stage_a2(s):
            hi = s % 2
            prev = None
            for c in range(4):
                bk = fm_chunk(C_Q + c * 128, "q", hi, W)
                rope_copy(bk, W, (c + 1) % 2)
                if prev is not None:
                    rope_rest(W, prev[0], prev[1], slot=prev[2])
                prev = (qT[:, c, :], [b_qT[c]], (c + 1) % 2)
                yield ("pf" if (s == 0 and c == 1) else None)
            rope_rest(W, prev[0], prev[1], slot=prev[2])
            if s == 0:
                halo_pu()
            yield
            pool_a(s, 0, hi)()
            yield

        def chain(*gs):
            for g in gs:
                if g is not None:
                    yield from g

        def stage_b(s):
            hi = s % 2
            gt0 = 1 + 4 * s
            pend_fin = None
            pfin = None
            z_part(hi, 0, 0)
            pool_b(0, hi, hi)
            yield
            for t in range(4):
                p0 = cnt["att"] % 2
                cnt["att"] += 2
                att_scores_both(s, t, p0)
                if t > 0:
                    z_part(hi, t, 0)
                if pend_fin is not None:
                    att_finish_pe(*pend_fin)
                    pend_fin = None
                if pfin is not None:
                    pfin()
                    pfin = None
                yield
                if t > 0:
                    pool_b(t, hi, hi)
                yield ("qfree" if t == 3 else None)
                oa = alloc2()
                att_pv(s, t, 0, p0, oa)
                pfin = pool_a(s, t + 1, hi) if t < 3 else None
                yield
                att_pv(s, t, 1, 1 - p0, oa)
                gi = cnt["fin"] % 2
                cnt["fin"] += 1
                att_finish_dve(oa, 0, gi)
                pend_fin = (hi, t, gi)
                yield
            yield
            att_finish_pe(*pend_fin)
            yield

        def stage_m(s, half=None):
            hi = s % 2
            c0, n = (0, W) if half is None else (half * 256, 256)
            mbufs = (lambda oc: [b_mT[oc], b_mTh[0][oc], b_mTh[1][oc]]) if half is None else \
                (lambda oc: [b_mTh[half][oc], b_mT[oc]])

            def gate_chunk(col):
                bk = alloc1()
                for kc in range(8):
                    op("pe", lambda e, kc=kc: e.matmul(bank(bk, n), lhsT=w_in[:, kc, col:col + 128],
                                                       rhs=hT[hi][:, kc, c0:c0 + n], start=(kc == 0), stop=(kc == 7)),
                       reads=[b_w["g0"], b_w["g1"]] + b_hT[hi], writes=[b_bank[bk]])
                return bk

            for oc in range(8):
                if oc == 2:
                    yield "pf"
                b0 = gate_chunk(C_G0 + oc * 128)
                op("act", lambda e, b0=b0: e.activation(out=sg0[:, 0:n], in_=bank(b0, n), func=AF.Tanh, scale=0.5),
                   reads=[b_bank[b0]], writes=[b_sg0])
                rel(b0)
                yield
                b1 = gate_chunk(C_G1 + oc * 128)
                op("act", lambda e, b1=b1: e.activation(out=sg1[:, 0:n], in_=bank(b1, n), func=AF.Tanh, scale=0.5),
                   reads=[b_bank[b1]], writes=[b_sg1])
                rel(b1)
                yield
                bp = alloc1()
                for kc in range(4):
                    op("pe", lambda e, kc=kc, bp=bp, oc=oc: e.matmul(bank(bp, n), lhsT=w_bp[:, kc, oc * 128:(oc + 1) * 128],
                                                              rhs=gpT[hi][:, kc, c0:c0 + n], start=(kc == 0), stop=(kc == 3)),
                       reads=[b_w["bp"]] + b_gpT[hi], writes=[b_bank[bp]])
                ba = alloc1()
                for kc in range(4):
                    op("pe", lambda e, kc=kc, ba=ba, oc=oc: e.matmul(bank(ba, n), lhsT=w_ba[:, kc, oc * 128:(oc + 1) * 128],
                                                              rhs=goT[hi][:, kc, c0:c0 + n], start=(kc == 0), stop=(kc == 3)),
                       reads=[b_w["ba"]] + b_goT[hi], writes=[b_bank[ba]])
                op("dve", lambda e, bp=bp: e.scalar_tensor_tensor(out=sg0[:, 0:n], in0=sg0[:, 0:n], scalar=1.0,
                                                                  in1=bank(bp, n), op0=ALU.add, op1=ALU.mult),
                   reads=[b_sg0, b_bank[bp]], writes=[b_sg0])
                rel(bp)
                op("dve", lambda e, ba=ba: e.scalar_tensor_tensor(out=sg1[:, 0:n], in0=sg1[:, 0:n], scalar=1.0,
                                                                  in1=bank(ba, n), op0=ALU.add, op1=ALU.mult),
                   reads=[b_sg1, b_bank[ba]], writes=[b_sg1])
                rel(ba)
                op("pool", lambda e, oc=oc: e.tensor_tensor(out=mT[:, oc, c0:c0 + n], in0=sg0[:, 0:n], in1=sg1[:, 0:n],
                                                            op=ALU.add),
                   reads=[b_sg0, b_sg1], writes=mbufs(oc))
                yield
            if s == 0:
                tap("mT0", mT[:, :, :], b_mT)

        b_ro_x = [[B(f"rox{i}_{k}") for k in range(2)] for i in range(2)]

        def ro_bufs(s):
            lst = [(ro[0][:, :], b_ro[0], []), (ro[1][:, :], b_ro[1], [])]
            if s >= NS - 2:
                hx = (NS - 2) % 2
                ext = hT[hx][:, :, :].rearrange("p k w -> p (k w)").bitcast(F32)
                for k in range(2):
                    lst.append((ext[:, k * 1024:(k + 1) * 1024], b_ro_x[0][k], b_hT[hx]))
            return lst

        def o_preload(s):
            bufs = ro_bufs(s)
            for t in range(4):
                rap, brb, extra = bufs[t]
                load(rap, x_rows(s, t), [brb] + extra)

        def stage_o(s, tiles=range(4), split=False, preloaded=False):
            junk, bjunk = (xs[0][:, :], b_xs[0]) if split else (sg0[:, :].bitcast(BF16), b_sg0)
            pend = None
            bufs = ro_bufs(s)
            four = len(bufs) == 4

            def epilogue(rap, brb, i, t):
                op("dve", lambda e: e.scalar_tensor_tensor(out=rap, in0=rap, scalar=st[:, i:i + 1],
                                                           in1=fg[:, :], op0=ALU.mult, op1=ALU.mult),
                   reads=[brb, b_st[i], b_fg], writes=[brb])
                bo = B(f"out{s}_{t}")
                b_out.append(bo)
                r0 = s * W + t * 128
                op("pool", lambda e: e.dma_start(out=out_c[r0:r0 + 128, :], in_=rap),
                   reads=[brb], writes=[bo], dma=True)

            if four and not preloaded:
                for t in tiles:
                    rap, brb, extra = bufs[t]
                    load(rap, x_rows(s, t), [brb] + extra)
            yield
            for t in tiles:
                if four:
                    rap, brb, _ = bufs[t]
                else:
                    ri = cnt["ro"] % 2
                    cnt["ro"] += 1
                    rap, brb, _ = bufs[ri]
                    load(rap, x_rows(s, t), [brb])
                yb = alloc2()
                for half in range(2):
                    for kc in range(8):
                        op("pe", lambda e, kc=kc, half=half, yb=yb, t=t: e.matmul(
                            bank(yb + half), lhsT=mT[:, kc, t * 128:(t + 1) * 128],
                            rhs=w_out[:, kc, half * 512:(half + 1) * 512], start=(kc == 0), stop=(kc == 7)),
                           reads=[b_w["out"]] + (b_mTh[t // 2] if split else b_mT), writes=[b_bank[yb + half]])
                    if half == 0:
                        if pend is not None:
                            epilogue(*pend)
                            pend = None
                        yield
                Y = ps[:, yb * 512:(yb + 2) * 512]
                op("dve", lambda e, Y=Y, rap=rap: e.scalar_tensor_tensor(out=rap, in0=Y, scalar=0.5,
                                                                         in1=rap, op0=ALU.mult, op1=ALU.add),
                   reads=[b_bank[yb], b_bank[yb + 1], brb], writes=[brb])
                rel(yb, yb + 1)
                i = rstd_of(rap, [brb], junk, [bjunk])
                pend = (rap, brb, i, t)
                yield
            epilogue(*pend)
            yield

        def drive(gens, gnext=None, on_qfree=None):
            gens = [(g, 1) if not isinstance(g, tuple) else g for g in gens if g is not None]
            while gens:
                for ent in list(gens):
                    g, k = ent
                    for _ in range(k):
                        try:
                            r = next(g)
                            if r == "pf" and gnext is not None:
                                next(gnext)
                            if r == "qfree" and on_qfree is not None:
                                gens.append((on_qfree, 1))
                        except StopIteration:
                            gens.remove(ent)
                            break

        weights_mid()
        ga1 = stage_a1(1)
        drive([chain(stage_a1(0), stage_a2(0))], ga1)
        weights_late()
        drive([stage_b(0), ga1], on_qfree=stage_a2(1))
        tap("hT0", hT[0][:, :, :], b_hT[0])
        tap("goT0", goT[0][:, :, :], b_goT[0])
        tap("gpT0", gpT[0][:, :, :], b_gpT[0])
        for s in range(NS - 1):
            if s + 2 < NS:
                ga = stage_a1(s + 2)
                drive([stage_b(s + 1), stage_m(s)], ga)
                drive([chain(ga, stage_a2(s + 2)), stage_o(s)])
            else:
                drive([stage_b(s + 1), (chain(stage_m(s), stage_o(s)), 2)])
        o_preload(NS - 1)
        drive([stage_m(NS - 1, half=0)])
        drive([stage_m(NS - 1, half=1), stage_o(NS - 1, tiles=range(0, 2), split=True, preloaded=True)])
        drive([stage_o(NS - 1, tiles=range(2, 4), split=True, preloaded=True)])

        op("sp", None, reads=b_out)
        op("pool", None, reads=b_out)

        keys = set()
        for stream in S.streams.values():
            for fn, wl, key, inc in stream:
                if key is not None:
                    keys.add(key)
        sems = {k: es.enter_context(nc.semaphore("s_" + (k if isinstance(k, str) else f"{k[0]}{k[1]}")))
                for k in sorted(keys, key=str)}

        def replay(eng, name):
            for fn, wl, key, inc in S.streams[name]:
                for k, v in wl:
                    eng.wait_ge(sems[k], v)
                if fn is not None:
                    fn(eng).then_inc(sems[key], inc)

        with nc.Block() as block:
            @block.tensor
            def _(e):
                replay(e, "pe")

            @block.scalar
            def _(e):
                replay(e, "act")

            @block.vector
            def _(e):
                replay(e, "dve")

            @block.gpsimd
            def _(e):
                replay(e, "pool")

            @block.sync
            def _(e):
                replay(e, "sp")
    return nc


def _host_layout(x, norm_gain, w_in, pool_w, pool_scale, attn_sinks, w_branch_pool, w_branch_attn, w_out, final_gain):
    f32 = np.float32
    perm = np.concatenate([
        1536 + np.arange(128), 1664 + np.arange(128), np.arange(512),
        np.concatenate([np.concatenate([1024 + c * 64 + np.arange(64), 1024 + (4 + c) * 64 + np.arange(64)])
                        for c in range(4)]),
        1792 + np.arange(512), 512 + np.arange(512), 2304 + np.arange(2048)])
    kmaj = lambda w, kc: np.ascontiguousarray(w.reshape(kc, 128, w.shape[1]).transpose(1, 0, 2))
    shared = {
        "w_in_l": kmaj(np.asarray(w_in[0], f32)[:, perm], 8),
        "w_bp_l": kmaj(np.asarray(w_branch_pool[0], f32), 4),
        "w_ba_l": kmaj(np.asarray(w_branch_attn[0], f32), 4),
        "w_out_l": kmaj(np.asarray(w_out[0], f32), 8),
        "pool_w_l": np.ascontiguousarray(np.asarray(pool_w[0], f32).transpose(1, 0, 2)),
        "fg_b": np.ascontiguousarray(np.broadcast_to(np.asarray(final_gain, f32)[None, :], (128, D))),
    }
    idr = np.zeros((128, 256), f32)
    idr[:, 0:128] = np.eye(128, dtype=f32)
    for a in range(128):
        idr[a, 128 + (a + 32 if a % 64 < 32 else a - 32)] = 1.0
    shared["idr"] = idr
    sidx, tidx = np.meshgrid(np.arange(128), np.arange(128), indexing="ij")
    mp = (tidx < sidx).astype(f32)
    mc = (tidx >= sidx).astype(f32)
    inv_freq = 10000.0 ** (-(np.arange(0, 64, 2, dtype=np.float64) / 64.0))
    x = np.asarray(x, f32)
    maps = []
    for c in range(NCORES):
        b, j = divmod(c, 4)
        p0 = j * T
        m = dict(shared)
        m["x_c"] = np.ascontiguousarray(x[b, p0:p0 + T])
        m["x_h"] = np.ascontiguousarray(x[b, p0 - 128:p0]) if p0 > 0 else np.zeros((128, D), f32)
        pos = np.arange(p0 - 128, p0 + T, dtype=np.float64)
        ang = (pos.astype(f32)[None, :] * inv_freq.astype(f32)[:, None]).astype(f32).astype(np.float64)
        cosr, sinr = np.cos(ang).astype(f32), np.sin(ang).astype(f32)
        m["cosT"] = np.ascontiguousarray(np.tile(cosr, (4, 1)))
        m["sinT"] = np.ascontiguousarray(np.concatenate([-sinr, sinr, -sinr, sinr], axis=0))
        m["masks"] = np.ascontiguousarray(np.concatenate([mp, mc, mp if p0 > 0 else np.zeros_like(mp), mc], axis=1))
        cst = np.zeros((128, NCST), f32)
        cst[:, CS_GAIN:CS_GAIN + 8] = np.asarray(norm_gain[0], f32).reshape(8, 128).T
        cst[:, CS_PSC:CS_PSC + 4] = np.asarray(pool_scale[0], f32).reshape(4, 128).T
        cst[:, CS_SINK:CS_SINK + 8] = np.asarray(attn_sinks[0], f32)[None, :]
        for g in range(4):
            wdw = 2 << g
            cntv = np.minimum(np.arange(p0, p0 + 16) + 1, wdw).astype(f32)
            cst[:, CS_INVC + g * 16:CS_INVC + (g + 1) * 16] = (1.0 / cntv)[None, :]
        cst[:, CS_NEGH] = -0.5
        m["cst"] = cst
        maps.append(m)
    return maps


_NC_CACHE = {}


def kernel(x, norm_gain, w_in, pool_w, pool_scale, attn_sinks, w_branch_pool, w_branch_attn, w_out, final_gain):
    maps = _host_layout(x, norm_gain, w_in, pool_w, pool_scale, attn_sinks, w_branch_pool, w_branch_attn, w_out,
                        final_gain)
    if "nc" not in _NC_CACHE:
        _NC_CACHE["nc"] = build_program()
    res = run_bass_kernel_spmd(_NC_CACHE["nc"], maps, core_ids=list(range(NCORES)))
    out = np.empty((2, SEQ, D), np.float32)
    for c in range(NCORES):
        b, j = divmod(c, 4)
        out[b, j * T:(j + 1) * T] = res.results[c]["out_c"]
    return out
```

```python
import contextlib
import sys
import numpy as np
import concourse.bass as bass
import concourse.mybir as mybir
from concourse.bass_utils import run_bass_kernel_spmd

F32 = mybir.dt.float32
BF16 = mybir.dt.bfloat16
AF = mybir.ActivationFunctionType
ALU = mybir.AluOpType

NCORES = 8
WSTEP = 1024
ATTACH_WAIT = ("pe",)
D = 1024
SEQ = 8192
T = 2048
W = 512
NS = T // W
IN_COLS = 4352
RMS_EPS = 1e-5
C_K, C_V, C_PU, C_Q, C_Z, C_PZ, C_G0, C_G1 = 0, 128, 256, 768, 1280, 1792, 2304, 3328
NDS = {"sp": 24, "pool": 20}
CS_GAIN, CS_PSC, CS_SINK, CS_INVC, CS_NEGH, NCST = 0, 8, 12, 20, 84, 85


_LAST = {}


class Buf:
    __slots__ = ("name", "w", "r")

    def __init__(self, name):
        self.name = name
        self.w = None
        self.r = {}


class Sched:
    def __init__(self):
        self.streams = {e: [] for e in ("pe", "act", "dve", "pool", "sp")}
        self.cnt = {}
        self.seen = {e: {} for e in self.streams}
        self.labels = {e: [] for e in self.streams}
        self.ndma = {}

    def op(self, eng, fn, reads=(), writes=(), dma=False):
        waits = {}
        fr = sys._getframe(1)
        label = f"{fr.f_code.co_name}:{fr.f_lineno}"

        def need(tok):
            if tok is None:
                return
            k, v = tok
            if v > waits.get(k, 0):
                waits[k] = v

        for b in reads:
            need(b.w)
        for b in writes:
            need(b.w)
            for k, v in b.r.items():
                need((k, v))
        if fn is None:
            key, inc, tok = None, 0, None
        else:
            if dma:
                n = self.ndma.get(eng, 0)
                self.ndma[eng] = n + 1
                key = ("d" + eng, n % NDS[eng])
                prev = self.cnt.get(key, 0)
                if prev:
                    need((key, prev))
                inc = 16
            else:
                key, inc = eng, 1
            val = self.cnt.get(key, 0) + inc
            self.cnt[key] = val
            tok = (key, val)
        seen = self.seen[eng]
        wl = []
        for k, v in waits.items():
            if k == "pe" and eng == "pe":
                continue
            if seen.get(k, 0) >= v:
                continue
            seen[k] = v
            wl.append((k, v))
        self.streams[eng].append((fn, wl, key, inc))
        self.labels[eng].append((label, tok, wl))
        if tok is not None:
            for b in reads:
                if b.r.get(tok[0], 0) < tok[1]:
                    b.r[tok[0]] = tok[1]
            for b in writes:
                b.w = tok
                b.r = {}
        return tok


def build_program(debug_taps=()):
    nc = bass.Bass("TRN2", target_bir_lowering=False)

    def din(name, shape):
        return nc.dram_tensor(name, list(shape), F32, kind="ExternalInput").ap()

    x_c = din("x_c", [T, D])
    x_h = din("x_h", [128, D])
    w_in_d = din("w_in_l", [128, 8, IN_COLS])
    w_bp_d = din("w_bp_l", [128, 4, D])
    w_ba_d = din("w_ba_l", [128, 4, D])
    w_out_d = din("w_out_l", [128, 8, D])
    pw_d = din("pool_w_l", [128, 4, 128])
    cst_d = din("cst", [128, NCST])
    fg_d = din("fg_b", [128, D])
    cos_d = din("cosT", [128, 128 + T])
    sin_d = din("sinT", [128, 128 + T])
    msk_d = din("masks", [128, 512])
    idr_d = din("idr", [128, 256])
    out_c = nc.dram_tensor("out_c", [T, D], F32, kind="ExternalOutput").ap()
    taps = {}
    for name, shape, tdt in debug_taps:
        taps[name] = nc.dram_tensor("tap_" + name, list(shape), tdt, kind="ExternalOutput").ap()

    S = Sched()
    _LAST['S'] = S
    es = contextlib.ExitStack()

    def sb(name, shape, dt):
        return es.enter_context(nc.sbuf_tensor("sb_" + name, list(shape), dt))

    with es:
        w_in = sb("w_in", [128, 8, IN_COLS], BF16)
        w_bp = sb("w_bp", [128, 4, D], BF16)
        w_ba = sb("w_ba", [128, 4, D], BF16)
        w_out = sb("w_out", [128, 8, D], BF16)
        pw = sb("pw", [128, 4, 128], BF16)
        cst = sb("cst", [128, NCST], F32)
        fg = sb("fg", [128, D], F32)
        msk = sb("msk", [128, 512], BF16)
        idb = sb("idb", [128, 128], BF16)
        rm = sb("rm", [128, 128], F32)
        esink = sb("esink", [128, 8], F32)
        psc = sb("psc", [128, 4], F32)
        xa = [sb(f"xa{i}", [128, D], F32) for i in range(2)]
        xs = [sb(f"xs{i}", [128, D], BF16) for i in range(2)]
        hT = [sb(f"hT{i}", [128, 8, W], BF16) for i in range(2)]
        qkf = sb("qkf", [128, W], F32)
        t2 = sb("t2", [128, W], F32)
        cs = sb("cs", [128, W], F32)
        sn = sb("sn", [128, W], F32)
        csh = sb("csh", [128, 128], F32)
        snh = sb("snh", [128, 128], F32)
        qT = sb("qT", [128, 4, W], BF16)
        kT = sb("kT", [128, 9 * 128], BF16)
        vA = sb("vA", [128, 9, 2, 65], BF16)
        sz = [sb("sz0", [128, 512], F32)]
        PT = [sb(f"PT{i}", [128, 2, 4, 128], BF16) for i in range(2)]
        go = [sb(f"go{i}", [128, 512], BF16) for i in range(2)]
        goT = [sb(f"goT{i}", [128, 4, W], BF16) for i in range(2)]
        pu = sb("pu", [128, 16 + W], F32)
        pA = sb("pA", [128, 16 + W], F32)
        pB = sb("pB", [128, 16 + W], F32)
        hist = sb("hist", [128, 4, 16], F32)
        pooled = [sb(f"pooled{i}", [128, W], BF16) for i in range(2)]
        spz = sb("spz", [128, W], F32)
        gpT = [sb(f"gpT{i}", [128, 4, W], BF16) for i in range(2)]
        sg0 = sb("sg0", [128, W], F32)
        sg1 = sb("sg1", [128, W], F32)
        mT = sb("mT", [128, 8, W], BF16)
        ro = [sb(f"ro{i}", [128, D], F32) for i in range(2)]
        st = sb("st", [128, 16], F32)
        den = sb("den", [128, 8], F32)
        rden = sb("rden", [128, 8], F32)
        ps = es.enter_context(nc.psum_tensor("ps", [128, 8 * 512], F32))

        B = Buf
        b_w = {k: B("w_" + k) for k in ("k", "v", "pu", "q", "z", "pz", "g0", "g1", "bp", "ba", "out", "pw")}
        b_cst, b_fg, b_msk, b_idb, b_rm, b_esink, b_psc = (B(n) for n in ("cst", "fg", "msk", "idb", "rm", "esink", "psc"))
        b_xa = [B("xa0"), B("xa1")]
        b_qkf, b_t2, b_cs, b_sn = (B(n) for n in ("qkf", "t2", "cs", "sn"))
        b_xs = [B("xs0"), B("xs1")]
        b_hT = [[B(f"hT{i}_{t}") for t in range(4)] for i in range(2)]
        b_qT = [B(f"qT{c}") for c in range(4)]
        b_kT = [B(f"kT{i}") for i in range(9)]
        b_vA = [B(f"vA{i}") for i in range(9)]
        b_vones = B("vones")
        b_sz = [B("sz0")]
        b_PT = [B("PT0"), B("PT1")]
        b_go = [B("go0"), B("go1")]
        b_goT = [[B(f"goT{i}_{t}") for t in range(4)] for i in range(2)]
        b_pu, b_pA, b_pB, b_spz = B("pu"), B("pA"), B("pB"), B("spz")
        b_hist = [B(f"hist{g}") for g in range(4)]
        b_pooled = [B("pooled0"), B("pooled1")]
        b_gpT = [[B(f"gpT{i}_{g}") for g in range(4)] for i in range(2)]
        b_sg0, b_sg1 = B("sg0"), B("sg1")
        b_mT = [B(f"mT{i}") for i in range(8)]
        b_mTh = [[B(f"mT{i}_h{h}") for i in range(8)] for h in range(2)]
        b_ro = [B("ro0"), B("ro1")]
        b_st = [B(f"st{i}") for i in range(16)]
        b_den, b_rden = B("den"), B("rden")
        b_bank = [B(f"bank{i}") for i in range(8)]
        b_out = []

        free_banks = list(range(8))

        def alloc1():
            for b_ in free_banks:
                if b_ >= 4:
                    free_banks.remove(b_)
                    return b_
            return free_banks.pop(0)

        def alloc2():
            best = None
            for p in range(0, 8, 2):
                if p in free_banks and (p + 1) in free_banks:
                    age = max(free_banks.index(p), free_banks.index(p + 1)) + (100 if p >= 4 else 0)
                    if best is None or age < best[0]:
                        best = (age, p)
            if best is None:
                raise RuntimeError("no free PSUM bank pair")
            p = best[1]
            free_banks.remove(p)
            free_banks.remove(p + 1)
            return p

        def rel(*bks):
            for b_ in bks:
                assert b_ not in free_banks
                free_banks.append(b_)

        def bank(i, n=512):
            return ps[:, i * 512:i * 512 + n]

        def bank_bf(i):
            return ps[:, i * 512:(i + 1) * 512].bitcast(BF16)

        op = S.op

        def tap(name, ap, rd):
            if name in taps:
                bo = B("tap_" + name)
                b_out.append(bo)
                op("sp", lambda e: e.dma_start(out=taps[name], in_=ap), reads=rd, writes=[bo], dma=True)

        def load(dst, src, wr, eng="sp"):
            op(eng, lambda e: e.dma_start(out=dst, in_=src), writes=wr, dma=True)

        def wload(dst, src, wr):
            op("pool", lambda e: e.dma_start(out=dst, in_=src, max_dma_last_dim=4096), writes=wr, dma=True)

        def wcols(keys, c0, n, step):
            for a_ in range(c0, c0 + n, step):
                m = min(step, c0 + n - a_)
                wload(w_in[:, :, a_:a_ + m], w_in_d[:, :, a_:a_ + m], [b_w[k_] for k_ in keys])

        load(xa[0][:, :], x_h[:, :], [b_xa[0]])
        load(cst[:, :], cst_d[:, :], [b_cst])
        load(rm[:, :], idr_d[:, 128:256], [b_rm])
        load(qkf[:, 0:128], idr_d[:, 0:128], [b_qkf])
        load(t2[:, 0:512], msk_d[:, :], [b_t2])
        b_csh, b_snh = B("csh"), B("snh")
        load(csh[:, :], cos_d[:, 0:128], [b_csh])
        load(snh[:, :], sin_d[:, 0:128], [b_snh])
        op("dve", lambda e: e.tensor_copy(out=idb[:, :], in_=qkf[:, 0:128]), reads=[b_qkf], writes=[b_idb])
        op("dve", lambda e: e.tensor_copy(out=msk[:, :], in_=t2[:, 0:512]), reads=[b_t2], writes=[b_msk])
        op("act", lambda e: e.activation(out=esink[:, :], in_=cst[:, CS_SINK:CS_SINK + 8], func=AF.Exp),
           reads=[b_cst], writes=[b_esink])
        op("dve", lambda e: e.tensor_scalar(out=psc[:, :], in0=cst[:, CS_PSC:CS_PSC + 4], scalar1=0.5, scalar2=None,
                                            op0=ALU.mult), reads=[b_cst], writes=[b_psc])
        op("pool", lambda e: e.memset(vA[:, :, :, :], 1.0), writes=[b_vones])
        wcols(["k", "v"], C_K, 256, 256)
        wcols(["q"], C_Q, 512, 512)
        wcols(["pu"], C_PU, 512, 512)
        wcols(["z"], C_Z, 512, 512)

        def weights_mid():
            wload(pw[:, :, :], pw_d[:, :, :], [b_w["pw"]])
            wcols(["pz"], C_PZ, 512, 512)

        def weights_late():
            wload(w_bp[:, :, :], w_bp_d[:, :, :], [b_w["bp"]])
            wload(w_ba[:, :, :], w_ba_d[:, :, :], [b_w["ba"]])
            wcols(["g0"], C_G0, 1024, WSTEP)
            wcols(["g1"], C_G1, 1024, WSTEP)
            wload(w_out[:, :, :], w_out_d[:, :, :], [b_w["out"]])
            load(fg[:, :], fg_d[:, :], [b_fg])

        stc = {"i": 0}
        cnt = {"rope": 0}

        def st_slot():
            i = stc["i"] % 16
            stc["i"] += 1
            return i

        def rstd_of(src_ap, src_bufs, junk_ap, junk_bufs):
            i = st_slot()
            j = st_slot()
            op("act", lambda e: e.activation(out=junk_ap, in_=src_ap, func=AF.Square, accum_out=st[:, i:i + 1]),
               reads=src_bufs, writes=junk_bufs + [b_st[i]])
            op("pool", lambda e: e.tensor_scalar(out=st[:, j:j + 1], in0=st[:, i:i + 1], scalar1=1.0 / D,
                                                 scalar2=RMS_EPS, op0=ALU.mult, op1=ALU.add),
               reads=[b_st[i]], writes=[b_st[j]])
            op("pool", lambda e: e.tensor_tensor(out=st[:, i:i + 1], in0=st[:, j:j + 1],
                                                 in1=cst[:, CS_NEGH:CS_NEGH + 1], op=ALU.pow),
               reads=[b_st[j], b_cst], writes=[b_st[i]])
            return i

        gain_b = cst[:, CS_GAIN:CS_GAIN + 8].unsqueeze(2).to_broadcast([128, 8, 128])

        def xbuf(xbuf_i):
            return (xa[xbuf_i], b_xa[xbuf_i]) if xbuf_i < 2 else (ro[xbuf_i - 2], b_ro[xbuf_i - 2])

        def norm_sq(xbuf_i, xsi, junk=None):
            xt, bx = xbuf(xbuf_i)
            if junk is None:
                return rstd_of(xt[:, :], [bx], xs[xsi][:, :], [b_xs[xsi]])
            return rstd_of(xt[:, :], [bx], PT[junk][:, :, :, :].rearrange("p a j q -> p (a j q)"), [b_PT[junk]])

        def norm_cp(xbuf_i, xsi, i):
            xt, bx = xbuf(xbuf_i)
            op("act", lambda e: e.activation(out=xs[xsi][:, :], in_=xt[:, :], func=AF.Copy, scale=st[:, i:i + 1]),
               reads=[bx, b_st[i]], writes=[b_xs[xsi]])

        def norm_pre(xbuf_i, xsi):
            norm_cp(xbuf_i, xsi, norm_sq(xbuf_i, xsi))

        def norm_pe(xsi, hi, t):
            bk = alloc1()
            tp = bank_bf(bk)
            for kc in range(8):
                op("pe", lambda e, kc=kc: e.transpose(out=tp[:, kc * 128:(kc + 1) * 128],
                                                      in_=xs[xsi][:, kc * 128:(kc + 1) * 128], identity=idb[:, :]),
                   reads=[b_xs[xsi], b_idb], writes=[b_bank[bk]])
            op("dve", lambda e: e.tensor_tensor(out=hT[hi][:, :, t * 128:(t + 1) * 128],
                                                in0=tp.rearrange("p (k t) -> p k t", k=8), in1=gain_b, op=ALU.mult),
               reads=[b_bank[bk], b_cst], writes=[b_hT[hi][t]])
            rel(bk)

        def norm_tile(xbuf_i, hi, t):
            norm_pre(xbuf_i, 0)
            norm_pe(0, hi, t)

        def fm_chunk(c0, wkey, hi, n):
            bk = alloc1()
            for kc in range(8):
                op("pe", lambda e, kc=kc: e.matmul(bank(bk, n), lhsT=w_in[:, kc, c0:c0 + 128], rhs=hT[hi][:, kc, 0:n],
                                                   start=(kc == 0), stop=(kc == 7)),
                   reads=[b_w[wkey]] + b_hT[hi][0:max(1, n // 128)], writes=[b_bank[bk]])
            return bk

        rope_ring = [(qkf, t2, b_qkf, b_t2), (pA, pB, b_pA, b_pB)]

        def rope_copy(bk, n, slot):
            qf, tf, bqf, btf = rope_ring[slot]
            op("act", lambda e: e.activation(out=qf[:, 0:n], in_=bank(bk, n), func=AF.Copy),
               reads=[b_bank[bk]], writes=[bqf])
            rel(bk)

        def rope_rest(n, dst_ap, dst_bufs, tabs=None, slot=0):
            cs_, sn_, bcs_, bsn_ = tabs if tabs is not None else (cs, sn, b_cs, b_sn)
            qf, tf, bqf, btf = rope_ring[slot]
            b2 = alloc1()
            op("pe", lambda e: e.matmul(bank(b2, n), lhsT=rm[:, :], rhs=qf[:, 0:n], start=True, stop=True),
               reads=[b_rm, bqf], writes=[b_bank[b2]])
            op("dve", lambda e: e.tensor_tensor(out=tf[:, 0:n], in0=bank(b2, n), in1=sn_[:, 0:n], op=ALU.mult),
               reads=[b_bank[b2], bsn_], writes=[btf])
            rel(b2)
            op("dve", lambda e: e.tensor_tensor(out=qf[:, 0:n], in0=qf[:, 0:n], in1=cs_[:, 0:n], op=ALU.mult),
               reads=[bqf, bcs_], writes=[bqf])
            op("dve", lambda e: e.tensor_tensor(out=dst_ap, in0=qf[:, 0:n], in1=tf[:, 0:n], op=ALU.add),
               reads=[bqf, btf], writes=dst_bufs)

        def rope_chunk(bk, n, dst_ap, dst_bufs, tabs=None, slot=0):
            rope_copy(bk, n, slot)
            rope_rest(n, dst_ap, dst_bufs, tabs, slot)

        def v_part(gt, hi, t):
            sl = gt % 9
            bv = alloc1()
            for kc in range(8):
                op("pe", lambda e, kc=kc: e.matmul(bank(bv, 128), lhsT=hT[hi][:, kc, t * 128:(t + 1) * 128],
                                                   rhs=w_in[:, kc, C_V:C_V + 128], start=(kc == 0), stop=(kc == 7)),
                   reads=[b_w["v"], b_hT[hi][t]], writes=[b_bank[bv]])
            op("act", lambda e: e.activation(out=vA[:, sl, :, 0:64],
                                             in_=bank(bv, 128).rearrange("p (g d) -> p g d", g=2), func=AF.Copy),
               reads=[b_bank[bv], b_vones], writes=[b_vA[sl]])
            rel(bv)

        def z_part(hi, t, zi):
            bz = alloc1()
            for kc in range(8):
                op("pe", lambda e, kc=kc: e.matmul(bank(bz), lhsT=hT[hi][:, kc, t * 128:(t + 1) * 128],
                                                   rhs=w_in[:, kc, C_Z:C_Z + 512], start=(kc == 0), stop=(kc == 7)),
                   reads=[b_w["z"], b_hT[hi][t]], writes=[b_bank[bz]])
            op("act", lambda e: e.activation(out=sz[zi][:, :], in_=bank(bz), func=AF.Tanh, scale=0.5),
               reads=[b_bank[bz]], writes=[b_sz[zi]])
            op("dve", lambda e: e.scalar_tensor_tensor(out=sz[zi][:, :], in0=sz[zi][:, :], scalar=1.0,
                                                       in1=bank(bz), op0=ALU.add, op1=ALU.mult),
               reads=[b_sz[zi], b_bank[bz]], writes=[b_sz[zi]])
            rel(bz)

        norm_tile(0, 1, 0)

        def halo_proj():
            bk = fm_chunk(C_K, "k", 1, 128)
            rope_chunk(bk, 128, kT[:, 0:128], [b_kT[0]], (csh, snh, b_csh, b_snh), slot=1)
            v_part(0, 1, 0)

        def halo_pu():
            for g in range(4):
                bk = fm_chunk(C_PU + g * 128, "pu", 1, 128)
                op("dve", lambda e, bk=bk, g=g: e.tensor_copy(out=hist[:, g, :], in_=bank(bk, 128)[:, 112:128]),
                   reads=[b_bank[bk]], writes=[b_hist[g]])
                rel(bk)

        def x_rows(s, t):
            r0 = s * W + t * 128
            return x_c[r0:r0 + 128, :]

        cnt.update({"att": 0, "fin": 0, "ro": 0, "xa": 1, "z": 0})

        def att_scores(s, t, g, pti):
            gt = 1 + 4 * s + t
            sb2 = alloc2()
            S2 = ps[:, sb2 * 512:(sb2 + 2) * 512]
            rq = qT[64 * g:64 * g + 64, :, t * 128:(t + 1) * 128]
            for kb in range(2):
                sl = (gt - 1 + kb) % 9
                op("pe", lambda e, kb=kb, sl=sl: e.matmul(
                    bank(sb2 + kb).rearrange("p (j q) -> p j q", j=4),
                    lhsT=kT[64 * g:64 * g + 64, sl * 128:(sl + 1) * 128], rhs=rq, start=True, stop=True),
                   reads=[b_kT[sl]] + b_qT, writes=[b_bank[sb2 + kb]])
            op("act", lambda e: e.activation(out=PT[pti][:, :, :, :].rearrange("p a j q -> p (a j q)"), in_=S2,
                                             func=AF.Exp, scale=0.125),
               reads=[b_bank[sb2], b_bank[sb2 + 1]], writes=[b_PT[pti]])
            rel(sb2, sb2 + 1)
            m0 = 256 if (s == 0 and t == 0) else 0
            mpair = msk[:, m0:m0 + 256].rearrange("p (a q) -> p a q", a=2).unsqueeze(2).to_broadcast([128, 2, 4, 128])
            op("dve", lambda e: e.tensor_tensor(out=PT[pti][:, :, :, :], in0=PT[pti][:, :, :, :], in1=mpair,
                                                op=ALU.mult),
               reads=[b_PT[pti], b_msk], writes=[b_PT[pti]])

        def att_pv(s, t, g, pti, oa):
            gt = 1 + 4 * s + t
            for j in range(4):
                oap = ps[:, (oa + g) * 512 + j * 65:(oa + g) * 512 + j * 65 + 65]
                for kb in range(2):
                    sl = (gt - 1 + kb) % 9
                    op("pe", lambda e, j=j, kb=kb, oap=oap, sl=sl: e.matmul(
                        oap, lhsT=PT[pti][:, kb, j, :], rhs=vA[:, sl, g, :], start=(kb == 0), stop=(kb == 1)),
                       reads=[b_PT[pti], b_vA[sl], b_vones], writes=[b_bank[oa + g]])

        def att_finish_dve(oa, zi, gi):
            O = ps[:, oa * 512:(oa + 2) * 512].rearrange("p (g c) -> p g c", g=2)[:, :, 0:260] \
                .rearrange("p g (j e) -> p g j e", e=65)
            op("dve", lambda e: e.tensor_tensor(out=den[:, :].rearrange("p (g j) -> p g j", g=2), in0=O[:, :, :, 64],
                                                in1=esink[:, :].rearrange("p (g j) -> p g j", g=2), op=ALU.add),
               reads=[b_bank[oa], b_bank[oa + 1], b_esink], writes=[b_den])
            op("dve", lambda e: e.reciprocal(out=rden[:, :], in_=den[:, :]), reads=[b_den], writes=[b_rden])
            rb = rden[:, :].rearrange("p (g j) -> p g j", g=2).unsqueeze(3).to_broadcast([128, 2, 4, 64])
            op("dve", lambda e: e.tensor_tensor(out=sz[zi][:, :].rearrange("p (g j d) -> p g j d", g=2, j=4),
                                                in0=sz[zi][:, :].rearrange("p (g j d) -> p g j d", g=2, j=4),
                                                in1=rb, op=ALU.mult),
               reads=[b_sz[zi], b_rden], writes=[b_sz[zi]])
            op("dve", lambda e: e.tensor_tensor(out=go[gi][:, :].rearrange("p (g j d) -> p g j d", g=2, j=4),
                                                in0=O[:, :, :, 0:64],
                                                in1=sz[zi][:, :].rearrange("p (g j d) -> p g j d", g=2, j=4),
                                                op=ALU.mult),
               reads=[b_bank[oa], b_bank[oa + 1], b_sz[zi]], writes=[b_go[gi]])
            rel(oa, oa + 1)

        def att_finish_pe(gi_buf, t, gi):
            bk = alloc1()
            tp = bank_bf(bk)
            for c in range(4):
                op("pe", lambda e, c=c: e.transpose(out=tp[:, c * 128:(c + 1) * 128],
                                                    in_=go[gi][:, c * 128:(c + 1) * 128], identity=idb[:, :]),
                   reads=[b_go[gi], b_idb], writes=[b_bank[bk]])
            op("act", lambda e: e.activation(out=goT[gi_buf][:, :, t * 128:(t + 1) * 128],
                                             in_=tp[:, 0:512].rearrange("p (c q) -> p c q", c=4), func=AF.Copy, scale=0.5),
               reads=[b_bank[bk]], writes=[b_goT[gi_buf][t]])
            rel(bk)

        def pool_a(s, g, hi):
            w = 2 << g
            pi = g % 2
            bk = fm_chunk(C_PU + g * 128, "pu", hi, W)
            op("act", lambda e: e.activation(out=pu[:, 16:16 + W], in_=bank(bk), func=AF.Copy),
               reads=[b_bank[bk]], writes=[b_pu])
            rel(bk)
            op("pool", lambda e: e.tensor_copy(out=pu[:, 0:16], in_=hist[:, g, :]), reads=[b_hist[g]], writes=[b_pu])
            op("pool", lambda e: e.tensor_copy(out=hist[:, g, :], in_=pu[:, W:W + 16]), reads=[b_pu], writes=[b_hist[g]])
            U = pu
            N = 16 + W
            op("pool", lambda e: e.tensor_tensor(out=pA[:, 1:N], in0=U[:, 1:N], in1=U[:, 0:N - 1], op=ALU.add),
               reads=[b_pu], writes=[b_pA])
            cur, curb, oth, othb = pA, b_pA, pB, b_pB
            sh, lo = 2, 1
            while sh < w:
                lo2 = lo + sh
                op("pool", lambda e, cur=cur, oth=oth, sh=sh, lo2=lo2: e.tensor_tensor(
                    out=oth[:, lo2:N], in0=cur[:, lo2:N], in1=cur[:, lo2 - sh:N - sh], op=ALU.add),
                   reads=[curb], writes=[othb])
                cur, curb, oth, othb = oth, othb, cur, curb
                sh, lo = sh * 2, lo2
            return lambda: pool_a_fin(s, g, pi, w, N, U, cur, curb, oth, othb)

        def pool_a_fin(s, g, pi, w, N, U, cur, curb, oth, othb):
            op("dve", lambda e, cur=cur: e.scalar_tensor_tensor(out=pooled[pi][:, :], in0=cur[:, 16:N], scalar=1.0 / w,
                                                                in1=U[:, 16:N], op0=ALU.mult, op1=ALU.subtract),
               reads=[curb, b_pu], writes=[b_pooled[pi]])
            if s == 0:
                ic = cst[:, CS_INVC + g * 16:CS_INVC + (g + 1) * 16]
                op("dve", lambda e, cur=cur, oth=oth: e.tensor_tensor(out=oth[:, 0:16], in0=cur[:, 16:32], in1=ic,
                                                                      op=ALU.mult),
                   reads=[curb, b_cst], writes=[othb])
                op("dve", lambda e, oth=oth: e.tensor_tensor(out=pooled[pi][:, 0:16], in0=oth[:, 0:16], in1=U[:, 16:32],
                                                             op=ALU.subtract),
                   reads=[othb, b_pu], writes=[b_pooled[pi]])

        def pool_b(g, hi, gb):
            pi = g % 2
            bz = fm_chunk(C_PZ + g * 128, "pz", hi, W)
            op("act", lambda e: e.activation(out=spz[:, :], in_=bank(bz), func=AF.Tanh, scale=0.5),
               reads=[b_bank[bz]], writes=[b_spz])
            op("dve", lambda e: e.scalar_tensor_tensor(out=spz[:, :], in0=spz[:, :], scalar=1.0, in1=bank(bz),
                                                       op0=ALU.add, op1=ALU.mult),
               reads=[b_spz, b_bank[bz]], writes=[b_spz])
            rel(bz)
            bm = alloc1()
            op("pe", lambda e: e.matmul(bank(bm), lhsT=pw[:, g, :], rhs=pooled[pi][:, :], start=True, stop=True),
               reads=[b_w["pw"], b_pooled[pi]], writes=[b_bank[bm]])
            op("dve", lambda e: e.scalar_tensor_tensor(out=gpT[gb][:, g, :], in0=bank(bm), scalar=psc[:, g:g + 1],
                                                       in1=spz[:, :], op0=ALU.mult, op1=ALU.mult),
               reads=[b_bank[bm], b_psc, b_spz], writes=[b_gpT[gb][g]])
            rel(bm)

        def stage_a1(s):
            hi = s % 2
            gt0 = 1 + 4 * s

            head = s <= 1

            def sq(t):
                if head:
                    xi = (1, 2, 3, 0)[t]
                    xt, bx = xbuf(xi)
                    load(xt[:, :], x_rows(s, t), [bx])
                    return (xi, t % 2, norm_sq(xi, t % 2, junk=(t - 2 if t >= 2 else None)))
                xi = cnt["xa"] % 2
                cnt["xa"] += 1
                load(xa[xi][:, :], x_rows(s, t), [b_xa[xi]])
                return (xi, t % 2, norm_sq(xi, t % 2))

            p0_, p1_ = sq(0), sq(1)
            norm_cp(*p0_)
            norm_cp(*p1_)
            pend = [sq(2), sq(3)] if head else None
            yield
            load(cs[:, :], cos_d[:, 128 + s * W:128 + (s + 1) * W], [b_cs])
            load(sn[:, :], sin_d[:, 128 + s * W:128 + (s + 1) * W], [b_sn])
            for t in range(4):
                norm_pe(t % 2, hi, t)
                pn = None
                if t + 2 < 4:
                    pn = pend[t] if head else sq(t + 2)
                if s > 0 and 0 < t < 3:
                    v_part(gt0 + t - 1, hi, t - 1)
                if pn is not None:
                    norm_cp(*pn)
                yield
            if s == 0:
                halo_proj()
                for t in range(4):
                    v_part(gt0 + t, hi, t)
                yield
            sl0 = gt0 % 9
            if s > 0:
                v_part(gt0 + 2, hi, 2)
                yield
            bk = fm_chunk(C_K, "k", hi, W)
            rope_copy(bk, W, 0)
            if s > 0:
                v_part(gt0 + 3, hi, 3)
            yield
            rope_rest(W, kT[:, sl0 * 128:sl0 * 128 + W], [b_kT[sl0 + i] for i in range(4)], slot=0)
            yield

        def stage_a2(s):
            hi = s % 2
            prev = None
            for c in range(4):
                bk = fm_chunk(C_Q + c * 128, "q", hi, W)
                rope_copy(bk, W, (c + 1) % 2)
                if prev is not None:
                    rope_rest(W, prev[0], prev[1], slot=prev[2])
                prev = (qT[:, c, :], [b_qT[c]], (c + 1) % 2)
                yield ("pf" if (s == 0 and c == 1) else None)
            rope_rest(W, prev[0], prev[1], slot=prev[2])
            if s == 0:
                halo_pu()
            yield
            pool_a(s, 0, hi)()
            yield

        def chain(*gs):
            for g in gs:
                if g is not None:
                    yield from g

        def stage_b(s):
            hi = s % 2
            gt0 = 1 + 4 * s
            pend_fin = None
            pfin = None
            z_part(hi, 0, 0)
            pool_b(0, hi, hi)
            yield
            for t in range(4):
                p0 = cnt["att"] % 2
                cnt["att"] += 2
                att_scores(s, t, 0, p0)
                if t > 0:
                    z_part(hi, t, 0)
                if pend_fin is not None:
                    att_finish_pe(*pend_fin)
                    pend_fin = None
                if pfin is not None:
                    pfin()
                    pfin = None
                yield
                att_scores(s, t, 1, 1 - p0)
                if t > 0:
                    pool_b(t, hi, hi)
                yield ("qfree" if t == 3 else None)
                oa = alloc2()
                att_pv(s, t, 0, p0, oa)
                pfin = pool_a(s, t + 1, hi) if t < 3 else None
                yield
                att_pv(s, t, 1, 1 - p0, oa)
                gi = cnt["fin"] % 2
                cnt["fin"] += 1
                att_finish_dve(oa, 0, gi)
                pend_fin = (hi, t, gi)
                yield
            yield
            att_finish_pe(*pend_fin)
            yield

        def stage_m(s, half=None):
            hi = s % 2
            c0, n = (0, W) if half is None else (half * 256, 256)
            mbufs = (lambda oc: [b_mT[oc], b_mTh[0][oc], b_mTh[1][oc]]) if half is None else \
                (lambda oc: [b_mTh[half][oc], b_mT[oc]])

            def gate_chunk(col):
                bk = alloc1()
                for kc in range(8):
                    op("pe", lambda e, kc=kc: e.matmul(bank(bk, n), lhsT=w_in[:, kc, col:col + 128],
                                                       rhs=hT[hi][:, kc, c0:c0 + n], start=(kc == 0), stop=(kc == 7)),
                       reads=[b_w["g0"], b_w["g1"]] + b_hT[hi], writes=[b_bank[bk]])
                return bk

            for oc in range(8):
                if oc == 2:
                    yield "pf"
                b0 = gate_chunk(C_G0 + oc * 128)
                op("act", lambda e, b0=b0: e.activation(out=sg0[:, 0:n], in_=bank(b0, n), func=AF.Tanh, scale=0.5),
                   reads=[b_bank[b0]], writes=[b_sg0])
                rel(b0)
                yield
                b1 = gate_chunk(C_G1 + oc * 128)
                op("act", lambda e, b1=b1: e.activation(out=sg1[:, 0:n], in_=bank(b1, n), func=AF.Tanh, scale=0.5),
                   reads=[b_bank[b1]], writes=[b_sg1])
                rel(b1)
                yield
                bp = alloc1()
                for kc in range(4):
                    op("pe", lambda e, kc=kc, bp=bp, oc=oc: e.matmul(bank(bp, n), lhsT=w_bp[:, kc, oc * 128:(oc + 1) * 128],
                                                              rhs=gpT[hi][:, kc, c0:c0 + n], start=(kc == 0), stop=(kc == 3)),
                       reads=[b_w["bp"]] + b_gpT[hi], writes=[b_bank[bp]])
                ba = alloc1()
                for kc in range(4):
                    op("pe", lambda e, kc=kc, ba=ba, oc=oc: e.matmul(bank(ba, n), lhsT=w_ba[:, kc, oc * 128:(oc + 1) * 128],
                                                              rhs=goT[hi][:, kc, c0:c0 + n], start=(kc == 0), stop=(kc == 3)),
                       reads=[b_w["ba"]] + b_goT[hi], writes=[b_bank[ba]])
                op("dve", lambda e, bp=bp: e.scalar_tensor_tensor(out=sg0[:, 0:n], in0=sg0[:, 0:n], scalar=1.0,
                                                                  in1=bank(bp, n), op0=ALU.add, op1=ALU.mult),
                   reads=[b_sg0, b_bank[bp]], writes=[b_sg0])
                rel(bp)
                op("dve", lambda e, ba=ba: e.scalar_tensor_tensor(out=sg1[:, 0:n], in0=sg1[:, 0:n], scalar=1.0,
                                                                  in1=bank(ba, n), op0=ALU.add, op1=ALU.mult),
                   reads=[b_sg1, b_bank[ba]], writes=[b_sg1])
                rel(ba)
                op("pool", lambda e, oc=oc: e.tensor_tensor(out=mT[:, oc, c0:c0 + n], in0=sg0[:, 0:n], in1=sg1[:, 0:n],
                                                            op=ALU.add),
                   reads=[b_sg0, b_sg1], writes=mbufs(oc))
                yield
            if s == 0:
                tap("mT0", mT[:, :, :], b_mT)

        b_ro_x = [[B(f"rox{i}_{k}") for k in range(2)] for i in range(2)]

        def ro_bufs(s):
            lst = [(ro[0][:, :], b_ro[0], []), (ro[1][:, :], b_ro[1], [])]
            if s >= NS - 2:
                hx = (NS - 2) % 2
                ext = hT[hx][:, :, :].rearrange("p k w -> p (k w)").bitcast(F32)
                for k in range(2):
                    lst.append((ext[:, k * 1024:(k + 1) * 1024], b_ro_x[0][k], b_hT[hx]))
            return lst

        def o_preload(s):
            bufs = ro_bufs(s)
            for t in range(4):
                rap, brb, extra = bufs[t]
                load(rap, x_rows(s, t), [brb] + extra)

        def stage_o(s, tiles=range(4), split=False, preloaded=False):
            junk, bjunk = (xs[0][:, :], b_xs[0]) if split else (sg0[:, :].bitcast(BF16), b_sg0)
            pend = None
            bufs = ro_bufs(s)
            four = len(bufs) == 4

            def epilogue(rap, brb, i, t):
                op("dve", lambda e: e.scalar_tensor_tensor(out=rap, in0=rap, scalar=st[:, i:i + 1],
                                                           in1=fg[:, :], op0=ALU.mult, op1=ALU.mult),
                   reads=[brb, b_st[i], b_fg], writes=[brb])
                bo = B(f"out{s}_{t}")
                b_out.append(bo)
                r0 = s * W + t * 128
                op("pool", lambda e: e.dma_start(out=out_c[r0:r0 + 128, :], in_=rap),
                   reads=[brb], writes=[bo], dma=True)

            if four and not preloaded:
                for t in tiles:
                    rap, brb, extra = bufs[t]
                    load(rap, x_rows(s, t), [brb] + extra)
            yield
            for t in tiles:
                if four:
                    rap, brb, _ = bufs[t]
                else:
                    ri = cnt["ro"] % 2
                    cnt["ro"] += 1
                    rap, brb, _ = bufs[ri]
                    load(rap, x_rows(s, t), [brb])
                yb = alloc2()
                for half in range(2):
                    for kc in range(8):
                        op("pe", lambda e, kc=kc, half=half, yb=yb, t=t: e.matmul(
                            bank(yb + half), lhsT=mT[:, kc, t * 128:(t + 1) * 128],
                            rhs=w_out[:, kc, half * 512:(half + 1) * 512], start=(kc == 0), stop=(kc == 7)),
                           reads=[b_w["out"]] + (b_mTh[t // 2] if split else b_mT), writes=[b_bank[yb + half]])
                    if half == 0:
                        if pend is not None:
                            epilogue(*pend)
                            pend = None
                        yield
                Y = ps[:, yb * 512:(yb + 2) * 512]
                op("dve", lambda e, Y=Y, rap=rap: e.scalar_tensor_tensor(out=rap, in0=Y, scalar=0.5,
                                                                         in1=rap, op0=ALU.mult, op1=ALU.add),
                   reads=[b_bank[yb], b_bank[yb + 1], brb], writes=[brb])
                rel(yb, yb + 1)
                i = rstd_of(rap, [brb], junk, [bjunk])
                pend = (rap, brb, i, t)
                yield
            epilogue(*pend)
            yield

        def drive(gens, gnext=None, on_qfree=None):
            gens = [(g, 1) if not isinstance(g, tuple) else g for g in gens if g is not None]
            while gens:
                for ent in list(gens):
                    g, k = ent
                    for _ in range(k):
                        try:
                            r = next(g)
                            if r == "pf" and gnext is not None:
                                next(gnext)
                            if r == "qfree" and on_qfree is not None:
                                gens.append((on_qfree, 1))
                        except StopIteration:
                            gens.remove(ent)
                            break

        weights_mid()
        ga1 = stage_a1(1)
        drive([chain(stage_a1(0), stage_a2(0))], ga1)
        weights_late()
        drive([stage_b(0), ga1], on_qfree=stage_a2(1))
        tap("hT0", hT[0][:, :, :], b_hT[0])
        tap("goT0", goT[0][:, :, :], b_goT[0])
        tap("gpT0", gpT[0][:, :, :], b_gpT[0])
        for s in range(NS - 1):
            if s + 2 < NS:
                ga = stage_a1(s + 2)
                drive([stage_b(s + 1), stage_m(s)], ga)
                drive([chain(ga, stage_a2(s + 2)), stage_o(s)])
            else:
                drive([stage_b(s + 1), (chain(stage_m(s), stage_o(s)), 2)])
        o_preload(NS - 1)
        drive([stage_m(NS - 1, half=0)])
        drive([stage_m(NS - 1, half=1), stage_o(NS - 1, tiles=range(0, 2), split=True, preloaded=True)])
        drive([stage_o(NS - 1, tiles=range(2, 4), split=True, preloaded=True)])

        op("sp", None, reads=b_out)
        op("pool", None, reads=b_out)

        keys = set()
        for stream in S.streams.values():
            for fn, wl, key, inc in stream:
                if key is not None:
                    keys.add(key)
        sems = {k: es.enter_context(nc.semaphore("s_" + (k if isinstance(k, str) else f"{k[0]}{k[1]}")))
                for k in sorted(keys, key=str)}

        def replay(eng, name):
            for fn, wl, key, inc in S.streams[name]:
                attach = None
                if fn is not None and wl and name in ATTACH_WAIT:
                    attach = wl[-1]
                    wl = wl[:-1]
                for k, v in wl:
                    eng.wait_ge(sems[k], v)
                if fn is not None:
                    ins = fn(eng)
                    if attach is not None:
                        ins.wait_op(sems[attach[0]], attach[1], "sem-ge")
                    ins.then_inc(sems[key], inc)

        with nc.Block() as block:
            @block.tensor
            def _(e):
                replay(e, "pe")

            @block.scalar
            def _(e):
                replay(e, "act")

            @block.vector
            def _(e):
                replay(e, "dve")

            @block.gpsimd
            def _(e):
                replay(e, "pool")

            @block.sync
            def _(e):
                replay(e, "sp")
    return nc


def _host_layout(x, norm_gain, w_in, pool_w, pool_scale, attn_sinks, w_branch_pool, w_branch_attn, w_out, final_gain):
    f32 = np.float32
    perm = np.concatenate([
        1536 + np.arange(128), 1664 + np.arange(128), np.arange(512),
        np.concatenate([np.concatenate([1024 + c * 64 + np.arange(64), 1024 + (4 + c) * 64 + np.arange(64)])
                        for c in range(4)]),
        1792 + np.arange(512), 512 + np.arange(512), 2304 + np.arange(2048)])
    kmaj = lambda w, kc: np.ascontiguousarray(w.reshape(kc, 128, w.shape[1]).transpose(1, 0, 2))
    shared = {
        "w_in_l": kmaj(np.asarray(w_in[0], f32)[:, perm], 8),
        "w_bp_l": kmaj(np.asarray(w_branch_pool[0], f32), 4),
        "w_ba_l": kmaj(np.asarray(w_branch_attn[0], f32), 4),
        "w_out_l": kmaj(np.asarray(w_out[0], f32), 8),
        "pool_w_l": np.ascontiguousarray(np.asarray(pool_w[0], f32).transpose(1, 0, 2)),
        "fg_b": np.ascontiguousarray(np.broadcast_to(np.asarray(final_gain, f32)[None, :], (128, D))),
    }
    idr = np.zeros((128, 256), f32)
    idr[:, 0:128] = np.eye(128, dtype=f32)
    for a in range(128):
        idr[a, 128 + (a + 32 if a % 64 < 32 else a - 32)] = 1.0
    shared["idr"] = idr
    sidx, tidx = np.meshgrid(np.arange(128), np.arange(128), indexing="ij")
    mp = (tidx < sidx).astype(f32)
    mc = (tidx >= sidx).astype(f32)
    inv_freq = 10000.0 ** (-(np.arange(0, 64, 2, dtype=np.float64) / 64.0))
    x = np.asarray(x, f32)
    maps = []
    for c in range(NCORES):
        b, j = divmod(c, 4)
        p0 = j * T
        m = dict(shared)
        m["x_c"] = np.ascontiguousarray(x[b, p0:p0 + T])
        m["x_h"] = np.ascontiguousarray(x[b, p0 - 128:p0]) if p0 > 0 else np.zeros((128, D), f32)
        pos = np.arange(p0 - 128, p0 + T, dtype=np.float64)
        ang = (pos.astype(f32)[None, :] * inv_freq.astype(f32)[:, None]).astype(f32).astype(np.float64)
        cosr, sinr = np.cos(ang).astype(f32), np.sin(ang).astype(f32)
        m["cosT"] = np.ascontiguousarray(np.tile(cosr, (4, 1)))
        m["sinT"] = np.ascontiguousarray(np.concatenate([-sinr, sinr, -sinr, sinr], axis=0))
        m["masks"] = np.ascontiguousarray(np.concatenate([mp, mc, mp if p0 > 0 else np.zeros_like(mp), mc], axis=1))
        cst = np.zeros((128, NCST), f32)
        cst[:, CS_GAIN:CS_GAIN + 8] = np.asarray(norm_gain[0], f32).reshape(8, 128).T
        cst[:, CS_PSC:CS_PSC + 4] = np.asarray(pool_scale[0], f32).reshape(4, 128).T
        cst[:, CS_SINK:CS_SINK + 8] = np.asarray(attn_sinks[0], f32)[None, :]
        for g in range(4):
            wdw = 2 << g
            cntv = np.minimum(np.arange(p0, p0 + 16) + 1, wdw).astype(f32)
            cst[:, CS_INVC + g * 16:CS_INVC + (g + 1) * 16] = (1.0 / cntv)[None, :]
        cst[:, CS_NEGH] = -0.5
        m["cst"] = cst
        maps.append(m)
    return maps


_NC_CACHE = {}


def kernel(x, norm_gain, w_in, pool_w, pool_scale, attn_sinks, w_branch_pool, w_branch_attn, w_out, final_gain):
    maps = _host_layout(x, norm_gain, w_in, pool_w, pool_scale, attn_sinks, w_branch_pool, w_branch_attn, w_out,
                        final_gain)
    if "nc" not in _NC_CACHE:
        _NC_CACHE["nc"] = build_program()
    res = run_bass_kernel_spmd(_NC_CACHE["nc"], maps, core_ids=list(range(NCORES)))
    out = np.empty((2, SEQ, D), np.float32)
    for c in range(NCORES):
        b, j = divmod(c, 4)
        out[b, j * T:(j + 1) * T] = res.results[c]["out_c"]
    return out
```

```python
import contextlib
import sys
import numpy as np
import concourse.bass as bass
import concourse.mybir as mybir
from concourse.bass_utils import run_bass_kernel_spmd

F32 = mybir.dt.float32
BF16 = mybir.dt.bfloat16
AF = mybir.ActivationFunctionType
ALU = mybir.AluOpType

NCORES = 8
WSTEP = 1024
D = 1024
SEQ = 8192
T = 2048
W = 512
NS = T // W
IN_COLS = 4352
RMS_EPS = 1e-5
C_K, C_V, C_PU, C_Q, C_Z, C_PZ, C_G0, C_G1 = 0, 128, 256, 768, 1280, 1792, 2304, 3328
NDS = {"sp": 24, "pool": 20}
CS_GAIN, CS_PSC, CS_SINK, CS_INVC, CS_NEGH, NCST = 0, 8, 12, 20, 84, 85


_LAST = {}


class Buf:
    __slots__ = ("name", "w", "r")

    def __init__(self, name):
        self.name = name
        self.w = None
        self.r = {}


class Sched:
    def __init__(self):
        self.streams = {e: [] for e in ("pe", "act", "dve", "pool", "sp")}
        self.cnt = {}
        self.seen = {e: {} for e in self.streams}
        self.labels = {e: [] for e in self.streams}
        self.ndma = {}

    def op(self, eng, fn, reads=(), writes=(), dma=False):
        waits = {}
        fr = sys._getframe(1)
        label = f"{fr.f_code.co_name}:{fr.f_lineno}"

        def need(tok):
            if tok is None:
                return
            k, v = tok
            if v > waits.get(k, 0):
                waits[k] = v

        for b in reads:
            need(b.w)
        for b in writes:
            need(b.w)
            for k, v in b.r.items():
                need((k, v))
        if fn is None:
            key, inc, tok = None, 0, None
        else:
            if dma:
                n = self.ndma.get(eng, 0)
                self.ndma[eng] = n + 1
                key = ("d" + eng, n % NDS[eng])
                prev = self.cnt.get(key, 0)
                if prev:
                    need((key, prev))
                inc = 16
            else:
                key, inc = eng, 1
            val = self.cnt.get(key, 0) + inc
            self.cnt[key] = val
            tok = (key, val)
        seen = self.seen[eng]
        wl = []
        for k, v in waits.items():
            if k == "pe" and eng == "pe":
                continue
            if seen.get(k, 0) >= v:
                continue
            seen[k] = v
            wl.append((k, v))
        self.streams[eng].append((fn, wl, key, inc))
        self.labels[eng].append((label, tok, wl))
        if tok is not None:
            for b in reads:
                if b.r.get(tok[0], 0) < tok[1]:
                    b.r[tok[0]] = tok[1]
            for b in writes:
                b.w = tok
                b.r = {}
        return tok


def build_program(debug_taps=()):
    nc = bass.Bass("TRN2", target_bir_lowering=False)

    def din(name, shape):
        return nc.dram_tensor(name, list(shape), F32, kind="ExternalInput").ap()

    x_c = din("x_c", [T, D])
    x_h = din("x_h", [128, D])
    w_in_d = din("w_in_l", [128, 8, IN_COLS])
    w_bp_d = din("w_bp_l", [128, 4, D])
    w_ba_d = din("w_ba_l", [128, 4, D])
    w_out_d = din("w_out_l", [128, 8, D])
    pw_d = din("pool_w_l", [128, 4, 128])
    cst_d = din("cst", [128, NCST])
    fg_d = din("fg_b", [128, D])
    cos_d = din("cosT", [128, 128 + T])
    sin_d = din("sinT", [128, 128 + T])
    msk_d = din("masks", [128, 512])
    idr_d = din("idr", [128, 256])
    out_c = nc.dram_tensor("out_c", [T, D], F32, kind="ExternalOutput").ap()
    taps = {}
    for name, shape, tdt in debug_taps:
        taps[name] = nc.dram_tensor("tap_" + name, list(shape), tdt, kind="ExternalOutput").ap()

    S = Sched()
    _LAST['S'] = S
    es = contextlib.ExitStack()

    def sb(name, shape, dt):
        return es.enter_context(nc.sbuf_tensor("sb_" + name, list(shape), dt))

    with es:
        w_in = sb("w_in", [128, 8, IN_COLS], BF16)
        w_bp = sb("w_bp", [128, 4, D], BF16)
        w_ba = sb("w_ba", [128, 4, D], BF16)
        w_out = sb("w_out", [128, 8, D], BF16)
        pw = sb("pw", [128, 4, 128], BF16)
        cst = sb("cst", [128, NCST], F32)
        fg = sb("fg", [128, D], F32)
        msk = sb("msk", [128, 512], BF16)
        idb = sb("idb", [128, 128], BF16)
        rm = sb("rm", [128, 128], F32)
        rmb = sb("rmb", [128, 128], BF16)
        esink = sb("esink", [128, 8], F32)
        psc = sb("psc", [128, 4], F32)
        xa = [sb(f"xa{i}", [128, D], F32) for i in range(2)]
        xs = [sb(f"xs{i}", [128, D], BF16) for i in range(2)]
        hT = [sb(f"hT{i}", [128, 8, W], BF16) for i in range(2)]
        qkf = sb("qkf", [128, W], F32)
        t2 = sb("t2", [128, W], F32)
        cs = sb("cs", [128, W], F32)
        sn = sb("sn", [128, W], F32)
        csh = sb("csh", [128, 128], F32)
        snh = sb("snh", [128, 128], F32)
        qT = sb("qT", [128, 4, W], BF16)
        kT = sb("kT", [128, 9 * 128], BF16)
        vA = sb("vA", [128, 9, 2, 65], BF16)
        sz = [sb("sz0", [128, 512], F32)]
        PT = [sb(f"PT{i}", [128, 2, 4, 128], BF16) for i in range(2)]
        go = [sb(f"go{i}", [128, 512], BF16) for i in range(2)]
        goT = [sb(f"goT{i}", [128, 4, W], BF16) for i in range(2)]
        pu = sb("pu", [128, 16 + W], F32)
        pA = sb("pA", [128, 16 + W], F32)
        pB = sb("pB", [128, 16 + W], F32)
        hist = sb("hist", [128, 4, 16], F32)
        pooled = [sb(f"pooled{i}", [128, W], BF16) for i in range(2)]
        spz = sb("spz", [128, W], F32)
        gpT = [sb(f"gpT{i}", [128, 4, W], BF16) for i in range(2)]
        sg0 = sb("sg0", [128, W], F32)
        sg1 = sb("sg1", [128, W], F32)
        mT = sb("mT", [128, 8, W], BF16)
        ro = [sb(f"ro{i}", [128, D], F32) for i in range(2)]
        st = sb("st", [128, 16], F32)
        den = sb("den", [128, 8], F32)
        rden = sb("rden", [128, 8], F32)
        ps = es.enter_context(nc.psum_tensor("ps", [128, 8 * 512], F32))

        B = Buf
        b_w = {k: B("w_" + k) for k in ("k", "v", "pu", "q", "z", "pz", "g0", "g1", "bp", "ba", "out", "pw")}
        b_cst, b_fg, b_msk, b_idb, b_rm, b_esink, b_psc = (B(n) for n in ("cst", "fg", "msk", "idb", "rm", "esink", "psc"))
        b_xa = [B("xa0"), B("xa1")]
        b_qkf, b_t2, b_cs, b_sn = (B(n) for n in ("qkf", "t2", "cs", "sn"))
        b_xs = [B("xs0"), B("xs1")]
        b_hT = [[B(f"hT{i}_{t}") for t in range(4)] for i in range(2)]
        b_qT = [B(f"qT{c}") for c in range(4)]
        b_kT = [B(f"kT{i}") for i in range(9)]
        b_vA = [B(f"vA{i}") for i in range(9)]
        b_vones = B("vones")
        b_sz = [B("sz0")]
        b_PT = [B("PT0"), B("PT1")]
        b_go = [B("go0"), B("go1")]
        b_goT = [[B(f"goT{i}_{t}") for t in range(4)] for i in range(2)]
        b_pu, b_pA, b_pB, b_spz = B("pu"), B("pA"), B("pB"), B("spz")
        b_hist = [B(f"hist{g}") for g in range(4)]
        b_pooled = [B("pooled0"), B("pooled1")]
        b_gpT = [[B(f"gpT{i}_{g}") for g in range(4)] for i in range(2)]
        b_sg0, b_sg1 = B("sg0"), B("sg1")
        b_mT = [B(f"mT{i}") for i in range(8)]
        b_mTh = [[B(f"mT{i}_h{h}") for i in range(8)] for h in range(2)]
        b_ro = [B("ro0"), B("ro1")]
        b_st = [B(f"st{i}") for i in range(16)]
        b_den, b_rden = B("den"), B("rden")
        b_bank = [B(f"bank{i}") for i in range(8)]
        b_out = []

        free_banks = list(range(8))

        def alloc1():
            for b_ in free_banks:
                if b_ >= 4:
                    free_banks.remove(b_)
                    return b_
            return free_banks.pop(0)

        def alloc2():
            best = None
            for p in range(0, 8, 2):
                if p in free_banks and (p + 1) in free_banks:
                    age = max(free_banks.index(p), free_banks.index(p + 1)) + (100 if p >= 4 else 0)
                    if best is None or age < best[0]:
                        best = (age, p)
            if best is None:
                raise RuntimeError("no free PSUM bank pair")
            p = best[1]
            free_banks.remove(p)
            free_banks.remove(p + 1)
            return p

        def rel(*bks):
            for b_ in bks:
                assert b_ not in free_banks
                free_banks.append(b_)

        def bank(i, n=512):
            return ps[:, i * 512:i * 512 + n]

        def bank_bf(i):
            return ps[:, i * 512:(i + 1) * 512].bitcast(BF16)

        op = S.op

        def tap(name, ap, rd):
            if name in taps:
                bo = B("tap_" + name)
                b_out.append(bo)
                op("sp", lambda e: e.dma_start(out=taps[name], in_=ap), reads=rd, writes=[bo], dma=True)

        def load(dst, src, wr, eng="sp"):
            op(eng, lambda e: e.dma_start(out=dst, in_=src), writes=wr, dma=True)

        def wload(dst, src, wr):
            op("pool", lambda e: e.dma_start(out=dst, in_=src, max_dma_last_dim=4096), writes=wr, dma=True)

        def wcols(keys, c0, n, step):
            for a_ in range(c0, c0 + n, step):
                m = min(step, c0 + n - a_)
                wload(w_in[:, :, a_:a_ + m], w_in_d[:, :, a_:a_ + m], [b_w[k_] for k_ in keys])

        load(xa[0][:, :], x_h[:, :], [b_xa[0]])
        load(cst[:, :], cst_d[:, :], [b_cst])
        load(rm[:, :], idr_d[:, 128:256], [b_rm])
        b_rmb = B("rmb")
        op("dve", lambda e: e.tensor_copy(out=rmb[:, :], in_=rm[:, :]), reads=[b_rm], writes=[b_rmb])
        load(qkf[:, 0:128], idr_d[:, 0:128], [b_qkf])
        load(t2[:, 0:512], msk_d[:, :], [b_t2])
        b_csh, b_snh = B("csh"), B("snh")
        load(csh[:, :], cos_d[:, 0:128], [b_csh])
        load(snh[:, :], sin_d[:, 0:128], [b_snh])
        op("dve", lambda e: e.tensor_copy(out=idb[:, :], in_=qkf[:, 0:128]), reads=[b_qkf], writes=[b_idb])
        op("dve", lambda e: e.tensor_copy(out=msk[:, :], in_=t2[:, 0:512]), reads=[b_t2], writes=[b_msk])
        op("act", lambda e: e.activation(out=esink[:, :], in_=cst[:, CS_SINK:CS_SINK + 8], func=AF.Exp),
           reads=[b_cst], writes=[b_esink])
        op("dve", lambda e: e.tensor_scalar(out=psc[:, :], in0=cst[:, CS_PSC:CS_PSC + 4], scalar1=0.5, scalar2=None,
                                            op0=ALU.mult), reads=[b_cst], writes=[b_psc])
        op("pool", lambda e: e.memset(vA[:, :, :, :], 1.0), writes=[b_vones])
        wcols(["k", "v"], C_K, 256, 256)
        wcols(["q"], C_Q, 512, 512)
        wcols(["pu"], C_PU, 512, 512)
        wcols(["z"], C_Z, 512, 512)

        def weights_mid():
            wload(pw[:, :, :], pw_d[:, :, :], [b_w["pw"]])
            wcols(["pz"], C_PZ, 512, 512)

        def weights_late():
            wload(w_bp[:, :, :], w_bp_d[:, :, :], [b_w["bp"]])
            wload(w_ba[:, :, :], w_ba_d[:, :, :], [b_w["ba"]])
            wcols(["g0"], C_G0, 1024, WSTEP)
            wcols(["g1"], C_G1, 1024, WSTEP)
            wload(w_out[:, :, :], w_out_d[:, :, :], [b_w["out"]])
            load(fg[:, :], fg_d[:, :], [b_fg])

        stc = {"i": 0}
        cnt = {"rope": 0}

        def st_slot():
            i = stc["i"] % 16
            stc["i"] += 1
            return i

        def rstd_of(src_ap, src_bufs, junk_ap, junk_bufs):
            i = st_slot()
            j = st_slot()
            op("act", lambda e: e.activation(out=junk_ap, in_=src_ap, func=AF.Square, accum_out=st[:, i:i + 1]),
               reads=src_bufs, writes=junk_bufs + [b_st[i]])
            op("pool", lambda e: e.tensor_scalar(out=st[:, j:j + 1], in0=st[:, i:i + 1], scalar1=1.0 / D,
                                                 scalar2=RMS_EPS, op0=ALU.mult, op1=ALU.add),
               reads=[b_st[i]], writes=[b_st[j]])
            op("pool", lambda e: e.tensor_tensor(out=st[:, i:i + 1], in0=st[:, j:j + 1],
                                                 in1=cst[:, CS_NEGH:CS_NEGH + 1], op=ALU.pow),
               reads=[b_st[j], b_cst], writes=[b_st[i]])
            return i

        gain_b = cst[:, CS_GAIN:CS_GAIN + 8].unsqueeze(2).to_broadcast([128, 8, 128])

        def xbuf(xbuf_i):
            return (xa[xbuf_i], b_xa[xbuf_i]) if xbuf_i < 2 else (ro[xbuf_i - 2], b_ro[xbuf_i - 2])

        def norm_sq(xbuf_i, xsi, junk=None):
            xt, bx = xbuf(xbuf_i)
            if junk is None:
                return rstd_of(xt[:, :], [bx], xs[xsi][:, :], [b_xs[xsi]])
            return rstd_of(xt[:, :], [bx], PT[junk][:, :, :, :].rearrange("p a j q -> p (a j q)"), [b_PT[junk]])

        def norm_cp(xbuf_i, xsi, i):
            xt, bx = xbuf(xbuf_i)
            op("act", lambda e: e.activation(out=xs[xsi][:, :], in_=xt[:, :], func=AF.Copy, scale=st[:, i:i + 1]),
               reads=[bx, b_st[i]], writes=[b_xs[xsi]])

        def norm_pre(xbuf_i, xsi):
            norm_cp(xbuf_i, xsi, norm_sq(xbuf_i, xsi))

        def norm_pe(xsi, hi, t):
            bk = alloc1()
            tp = bank_bf(bk)
            for kc in range(8):
                op("pe", lambda e, kc=kc: e.transpose(out=tp[:, kc * 128:(kc + 1) * 128],
                                                      in_=xs[xsi][:, kc * 128:(kc + 1) * 128], identity=idb[:, :]),
                   reads=[b_xs[xsi], b_idb], writes=[b_bank[bk]])
            op("dve", lambda e: e.tensor_tensor(out=hT[hi][:, :, t * 128:(t + 1) * 128],
                                                in0=tp.rearrange("p (k t) -> p k t", k=8), in1=gain_b, op=ALU.mult),
               reads=[b_bank[bk], b_cst], writes=[b_hT[hi][t]])
            rel(bk)

        def norm_tile(xbuf_i, hi, t):
            norm_pre(xbuf_i, 0)
            norm_pe(0, hi, t)

        def fm_chunk(c0, wkey, hi, n):
            bk = alloc1()
            for kc in range(8):
                op("pe", lambda e, kc=kc: e.matmul(bank(bk, n), lhsT=w_in[:, kc, c0:c0 + 128], rhs=hT[hi][:, kc, 0:n],
                                                   start=(kc == 0), stop=(kc == 7)),
                   reads=[b_w[wkey]] + b_hT[hi][0:max(1, n // 128)], writes=[b_bank[bk]])
            return bk

        rope_ring = [(qkf, t2, b_qkf, b_t2), (pA, pB, b_pA, b_pB)]

        def rope_copy(bk, n, slot, dst_ap, dst_bufs, tabs=None):
            cs_, sn_, bcs_, bsn_ = tabs if tabs is not None else (cs, sn, b_cs, b_sn)
            qf, tf, bqf, btf = rope_ring[slot]
            op("act", lambda e: e.activation(out=dst_ap, in_=bank(bk, n), func=AF.Copy),
               reads=[b_bank[bk]], writes=dst_bufs)
            op("dve", lambda e: e.tensor_tensor(out=qf[:, 0:n], in0=bank(bk, n), in1=cs_[:, 0:n], op=ALU.mult),
               reads=[b_bank[bk], bcs_] + dst_bufs, writes=[bqf])
            rel(bk)

        def rope_rest(n, dst_ap, dst_bufs, tabs=None, slot=0):
            cs_, sn_, bcs_, bsn_ = tabs if tabs is not None else (cs, sn, b_cs, b_sn)
            qf, tf, bqf, btf = rope_ring[slot]
            b2 = alloc1()
            op("pe", lambda e: e.matmul(bank(b2, n), lhsT=rmb[:, :], rhs=dst_ap, start=True, stop=True),
               reads=[b_rmb] + dst_bufs, writes=[b_bank[b2]])
            op("dve", lambda e: e.tensor_tensor(out=tf[:, 0:n], in0=bank(b2, n), in1=sn_[:, 0:n], op=ALU.mult),
               reads=[b_bank[b2], bsn_], writes=[btf])
            rel(b2)
            op("dve", lambda e: e.tensor_tensor(out=dst_ap, in0=qf[:, 0:n], in1=tf[:, 0:n], op=ALU.add),
               reads=[bqf, btf], writes=dst_bufs)

        def rope_chunk(bk, n, dst_ap, dst_bufs, tabs=None, slot=0):
            rope_copy(bk, n, slot, dst_ap, dst_bufs, tabs)
            rope_rest(n, dst_ap, dst_bufs, tabs, slot)

        def v_part(gt, hi, t):
            sl = gt % 9
            bv = alloc1()
            for kc in range(8):
                op("pe", lambda e, kc=kc: e.matmul(bank(bv, 128), lhsT=hT[hi][:, kc, t * 128:(t + 1) * 128],
                                                   rhs=w_in[:, kc, C_V:C_V + 128], start=(kc == 0), stop=(kc == 7)),
                   reads=[b_w["v"], b_hT[hi][t]], writes=[b_bank[bv]])
            op("act", lambda e: e.activation(out=vA[:, sl, :, 0:64],
                                             in_=bank(bv, 128).rearrange("p (g d) -> p g d", g=2), func=AF.Copy),
               reads=[b_bank[bv], b_vones], writes=[b_vA[sl]])
            rel(bv)

        def z_part(hi, t, zi):
            bz = alloc1()
            for kc in range(8):
                op("pe", lambda e, kc=kc: e.matmul(bank(bz), lhsT=hT[hi][:, kc, t * 128:(t + 1) * 128],
                                                   rhs=w_in[:, kc, C_Z:C_Z + 512], start=(kc == 0), stop=(kc == 7)),
                   reads=[b_w["z"], b_hT[hi][t]], writes=[b_bank[bz]])
            op("act", lambda e: e.activation(out=sz[zi][:, :], in_=bank(bz), func=AF.Tanh, scale=0.5),
               reads=[b_bank[bz]], writes=[b_sz[zi]])
            op("dve", lambda e: e.scalar_tensor_tensor(out=sz[zi][:, :], in0=sz[zi][:, :], scalar=1.0,
                                                       in1=bank(bz), op0=ALU.add, op1=ALU.mult),
               reads=[b_sz[zi], b_bank[bz]], writes=[b_sz[zi]])
            rel(bz)

        norm_tile(0, 1, 0)

        def halo_proj():
            bk = fm_chunk(C_K, "k", 1, 128)
            rope_chunk(bk, 128, kT[:, 0:128], [b_kT[0]], (csh, snh, b_csh, b_snh), slot=1)
            v_part(0, 1, 0)

        def halo_pu():
            for g in range(4):
                bk = fm_chunk(C_PU + g * 128, "pu", 1, 128)
                op("dve", lambda e, bk=bk, g=g: e.tensor_copy(out=hist[:, g, :], in_=bank(bk, 128)[:, 112:128]),
                   reads=[b_bank[bk]], writes=[b_hist[g]])
                rel(bk)

        def x_rows(s, t):
            r0 = s * W + t * 128
            return x_c[r0:r0 + 128, :]

        cnt.update({"att": 0, "fin": 0, "ro": 0, "xa": 1, "z": 0})

        def att_scores(s, t, g, pti):
            gt = 1 + 4 * s + t
            sb2 = alloc2()
            S2 = ps[:, sb2 * 512:(sb2 + 2) * 512]
            rq = qT[64 * g:64 * g + 64, :, t * 128:(t + 1) * 128]
            for kb in range(2):
                sl = (gt - 1 + kb) % 9
                op("pe", lambda e, kb=kb, sl=sl: e.matmul(
                    bank(sb2 + kb).rearrange("p (j q) -> p j q", j=4),
                    lhsT=kT[64 * g:64 * g + 64, sl * 128:(sl + 1) * 128], rhs=rq, start=True, stop=True),
                   reads=[b_kT[sl]] + b_qT, writes=[b_bank[sb2 + kb]])
            op("act", lambda e: e.activation(out=PT[pti][:, :, :, :].rearrange("p a j q -> p (a j q)"), in_=S2,
                                             func=AF.Exp, scale=0.125),
               reads=[b_bank[sb2], b_bank[sb2 + 1]], writes=[b_PT[pti]])
            rel(sb2, sb2 + 1)
            m0 = 256 if (s == 0 and t == 0) else 0
            mpair = msk[:, m0:m0 + 256].rearrange("p (a q) -> p a q", a=2).unsqueeze(2).to_broadcast([128, 2, 4, 128])
            op("dve", lambda e: e.tensor_tensor(out=PT[pti][:, :, :, :], in0=PT[pti][:, :, :, :], in1=mpair,
                                                op=ALU.mult),
               reads=[b_PT[pti], b_msk], writes=[b_PT[pti]])

        def att_pv(s, t, g, pti, oa):
            gt = 1 + 4 * s + t
            for j in range(4):
                oap = ps[:, (oa + g) * 512 + j * 65:(oa + g) * 512 + j * 65 + 65]
                for kb in range(2):
                    sl = (gt - 1 + kb) % 9
                    op("pe", lambda e, j=j, kb=kb, oap=oap, sl=sl: e.matmul(
                        oap, lhsT=PT[pti][:, kb, j, :], rhs=vA[:, sl, g, :], start=(kb == 0), stop=(kb == 1)),
                       reads=[b_PT[pti], b_vA[sl], b_vones], writes=[b_bank[oa + g]])

        def att_finish_dve(oa, zi, gi):
            O = ps[:, oa * 512:(oa + 2) * 512].rearrange("p (g c) -> p g c", g=2)[:, :, 0:260] \
                .rearrange("p g (j e) -> p g j e", e=65)
            op("dve", lambda e: e.tensor_tensor(out=den[:, :].rearrange("p (g j) -> p g j", g=2), in0=O[:, :, :, 64],
                                                in1=esink[:, :].rearrange("p (g j) -> p g j", g=2), op=ALU.add),
               reads=[b_bank[oa], b_bank[oa + 1], b_esink], writes=[b_den])
            op("dve", lambda e: e.reciprocal(out=rden[:, :], in_=den[:, :]), reads=[b_den], writes=[b_rden])
            rb = rden[:, :].rearrange("p (g j) -> p g j", g=2).unsqueeze(3).to_broadcast([128, 2, 4, 64])
            op("dve", lambda e: e.tensor_tensor(out=sz[zi][:, :].rearrange("p (g j d) -> p g j d", g=2, j=4),
                                                in0=sz[zi][:, :].rearrange("p (g j d) -> p g j d", g=2, j=4),
                                                in1=rb, op=ALU.mult),
               reads=[b_sz[zi], b_rden], writes=[b_sz[zi]])
            op("dve", lambda e: e.tensor_tensor(out=go[gi][:, :].rearrange("p (g j d) -> p g j d", g=2, j=4),
                                                in0=O[:, :, :, 0:64],
                                                in1=sz[zi][:, :].rearrange("p (g j d) -> p g j d", g=2, j=4),
                                                op=ALU.mult),
               reads=[b_bank[oa], b_bank[oa + 1], b_sz[zi]], writes=[b_go[gi]])
            rel(oa, oa + 1)

        def att_finish_pe(gi_buf, t, gi):
            bk = alloc1()
            tp = bank_bf(bk)
            for c in range(4):
                op("pe", lambda e, c=c: e.transpose(out=tp[:, c * 128:(c + 1) * 128],
                                                    in_=go[gi][:, c * 128:(c + 1) * 128], identity=idb[:, :]),
                   reads=[b_go[gi], b_idb], writes=[b_bank[bk]])
            op("act", lambda e: e.activation(out=goT[gi_buf][:, :, t * 128:(t + 1) * 128],
                                             in_=tp[:, 0:512].rearrange("p (c q) -> p c q", c=4), func=AF.Copy, scale=0.5),
               reads=[b_bank[bk]], writes=[b_goT[gi_buf][t]])
            rel(bk)

        def pool_a(s, g, hi):
            w = 2 << g
            pi = g % 2
            bk = fm_chunk(C_PU + g * 128, "pu", hi, W)
            op("act", lambda e: e.activation(out=pu[:, 16:16 + W], in_=bank(bk), func=AF.Copy),
               reads=[b_bank[bk]], writes=[b_pu])
            rel(bk)
            op("pool", lambda e: e.tensor_copy(out=pu[:, 0:16], in_=hist[:, g, :]), reads=[b_hist[g]], writes=[b_pu])
            op("pool", lambda e: e.tensor_copy(out=hist[:, g, :], in_=pu[:, W:W + 16]), reads=[b_pu], writes=[b_hist[g]])
            U = pu
            N = 16 + W
            op("pool", lambda e: e.tensor_tensor(out=pA[:, 1:N], in0=U[:, 1:N], in1=U[:, 0:N - 1], op=ALU.add),
               reads=[b_pu], writes=[b_pA])
            cur, curb, oth, othb = pA, b_pA, pB, b_pB
            sh, lo = 2, 1
            while sh < w:
                lo2 = lo + sh
                op("pool", lambda e, cur=cur, oth=oth, sh=sh, lo2=lo2: e.tensor_tensor(
                    out=oth[:, lo2:N], in0=cur[:, lo2:N], in1=cur[:, lo2 - sh:N - sh], op=ALU.add),
                   reads=[curb], writes=[othb])
                cur, curb, oth, othb = oth, othb, cur, curb
                sh, lo = sh * 2, lo2
            return lambda: pool_a_fin(s, g, pi, w, N, U, cur, curb, oth, othb)

        def pool_a_fin(s, g, pi, w, N, U, cur, curb, oth, othb):
            op("dve", lambda e, cur=cur: e.scalar_tensor_tensor(out=pooled[pi][:, :], in0=cur[:, 16:N], scalar=1.0 / w,
                                                                in1=U[:, 16:N], op0=ALU.mult, op1=ALU.subtract),
               reads=[curb, b_pu], writes=[b_pooled[pi]])
            if s == 0:
                ic = cst[:, CS_INVC + g * 16:CS_INVC + (g + 1) * 16]
                op("dve", lambda e, cur=cur, oth=oth: e.tensor_tensor(out=oth[:, 0:16], in0=cur[:, 16:32], in1=ic,
                                                                      op=ALU.mult),
                   reads=[curb, b_cst], writes=[othb])
                op("dve", lambda e, oth=oth: e.tensor_tensor(out=pooled[pi][:, 0:16], in0=oth[:, 0:16], in1=U[:, 16:32],
                                                             op=ALU.subtract),
                   reads=[othb, b_pu], writes=[b_pooled[pi]])

        def pool_b(g, hi, gb):
            pi = g % 2
            bz = fm_chunk(C_PZ + g * 128, "pz", hi, W)
            op("act", lambda e: e.activation(out=spz[:, :], in_=bank(bz), func=AF.Tanh, scale=0.5),
               reads=[b_bank[bz]], writes=[b_spz])
            op("dve", lambda e: e.scalar_tensor_tensor(out=spz[:, :], in0=spz[:, :], scalar=1.0, in1=bank(bz),
                                                       op0=ALU.add, op1=ALU.mult),
               reads=[b_spz, b_bank[bz]], writes=[b_spz])
            rel(bz)
            bm = alloc1()
            op("pe", lambda e: e.matmul(bank(bm), lhsT=pw[:, g, :], rhs=pooled[pi][:, :], start=True, stop=True),
               reads=[b_w["pw"], b_pooled[pi]], writes=[b_bank[bm]])
            op("dve", lambda e: e.scalar_tensor_tensor(out=gpT[gb][:, g, :], in0=bank(bm), scalar=psc[:, g:g + 1],
                                                       in1=spz[:, :], op0=ALU.mult, op1=ALU.mult),
               reads=[b_bank[bm], b_psc, b_spz], writes=[b_gpT[gb][g]])
            rel(bm)

        def stage_a1(s):
            hi = s % 2
            gt0 = 1 + 4 * s

            head = s <= 1

            def sq(t):
                if head:
                    xi = (1, 2, 3, 0)[t]
                    xt, bx = xbuf(xi)
                    load(xt[:, :], x_rows(s, t), [bx])
                    return (xi, t % 2, norm_sq(xi, t % 2, junk=(t - 2 if t >= 2 else None)))
                xi = cnt["xa"] % 2
                cnt["xa"] += 1
                load(xa[xi][:, :], x_rows(s, t), [b_xa[xi]])
                return (xi, t % 2, norm_sq(xi, t % 2))

            p0_, p1_ = sq(0), sq(1)
            norm_cp(*p0_)
            norm_cp(*p1_)
            pend = [sq(2), sq(3)] if head else None
            yield
            load(cs[:, :], cos_d[:, 128 + s * W:128 + (s + 1) * W], [b_cs])
            load(sn[:, :], sin_d[:, 128 + s * W:128 + (s + 1) * W], [b_sn])
            for t in range(4):
                norm_pe(t % 2, hi, t)
                pn = None
                if t + 2 < 4:
                    pn = pend[t] if head else sq(t + 2)
                if s > 0 and 0 < t < 3:
                    v_part(gt0 + t - 1, hi, t - 1)
                if pn is not None:
                    norm_cp(*pn)
                yield
            if s == 0:
                halo_proj()
                for t in range(4):
                    v_part(gt0 + t, hi, t)
                yield
            sl0 = gt0 % 9
            if s > 0:
                v_part(gt0 + 2, hi, 2)
                yield
            bk = fm_chunk(C_K, "k", hi, W)
            rope_copy(bk, W, 0, kT[:, sl0 * 128:sl0 * 128 + W], [b_kT[sl0 + i] for i in range(4)])
            if s > 0:
                v_part(gt0 + 3, hi, 3)
            yield
            rope_rest(W, kT[:, sl0 * 128:sl0 * 128 + W], [b_kT[sl0 + i] for i in range(4)], slot=0)
            yield

        def stage_a2(s):
            hi = s % 2
            prev = None
            for c in range(4):
                bk = fm_chunk(C_Q + c * 128, "q", hi, W)
                rope_copy(bk, W, (c + 1) % 2, qT[:, c, :], [b_qT[c]])
                if prev is not None:
                    rope_rest(W, prev[0], prev[1], slot=prev[2])
                prev = (qT[:, c, :], [b_qT[c]], (c + 1) % 2)
                yield ("pf" if (s == 0 and c == 1) else None)
            rope_rest(W, prev[0], prev[1], slot=prev[2])
            if s == 0:
                halo_pu()
            yield
            pool_a(s, 0, hi)()
            yield

        def chain(*gs):
            for g in gs:
                if g is not None:
                    yield from g

        def stage_b(s):
            hi = s % 2
            gt0 = 1 + 4 * s
            pend_fin = None
            pfin = None
            z_part(hi, 0, 0)
            pool_b(0, hi, hi)
            yield
            for t in range(4):
                p0 = cnt["att"] % 2
                cnt["att"] += 2
                att_scores(s, t, 0, p0)
                if t > 0:
                    z_part(hi, t, 0)
                if pend_fin is not None:
                    att_finish_pe(*pend_fin)
                    pend_fin = None
                if pfin is not None:
                    pfin()
                    pfin = None
                yield
                att_scores(s, t, 1, 1 - p0)
                if t > 0:
                    pool_b(t, hi, hi)
                yield ("qfree" if t == 3 else None)
                oa = alloc2()
                att_pv(s, t, 0, p0, oa)
                pfin = pool_a(s, t + 1, hi) if t < 3 else None
                yield
                att_pv(s, t, 1, 1 - p0, oa)
                gi = cnt["fin"] % 2
                cnt["fin"] += 1
                att_finish_dve(oa, 0, gi)
                pend_fin = (hi, t, gi)
                yield
            yield
            att_finish_pe(*pend_fin)
            yield

        def stage_m(s, half=None):
            hi = s % 2
            c0, n = (0, W) if half is None else (half * 256, 256)
            mbufs = (lambda oc: [b_mT[oc], b_mTh[0][oc], b_mTh[1][oc]]) if half is None else \
                (lambda oc: [b_mTh[half][oc], b_mT[oc]])

            def gate_chunk(col):
                bk = alloc1()
                for kc in range(8):
                    op("pe", lambda e, kc=kc: e.matmul(bank(bk, n), lhsT=w_in[:, kc, col:col + 128],
                                                       rhs=hT[hi][:, kc, c0:c0 + n], start=(kc == 0), stop=(kc == 7)),
                       reads=[b_w["g0"], b_w["g1"]] + b_hT[hi], writes=[b_bank[bk]])
                return bk

            for oc in range(8):
                if oc == 2:
                    yield "pf"
                b0 = gate_chunk(C_G0 + oc * 128)
                op("act", lambda e, b0=b0: e.activation(out=sg0[:, 0:n], in_=bank(b0, n), func=AF.Tanh, scale=0.5),
                   reads=[b_bank[b0]], writes=[b_sg0])
                rel(b0)
                yield
                b1 = gate_chunk(C_G1 + oc * 128)
                op("act", lambda e, b1=b1: e.activation(out=sg1[:, 0:n], in_=bank(b1, n), func=AF.Tanh, scale=0.5),
                   reads=[b_bank[b1]], writes=[b_sg1])
                rel(b1)
                yield
                bp = alloc1()
                for kc in range(4):
                    op("pe", lambda e, kc=kc, bp=bp, oc=oc: e.matmul(bank(bp, n), lhsT=w_bp[:, kc, oc * 128:(oc + 1) * 128],
                                                              rhs=gpT[hi][:, kc, c0:c0 + n], start=(kc == 0), stop=(kc == 3)),
                       reads=[b_w["bp"]] + b_gpT[hi], writes=[b_bank[bp]])
                ba = alloc1()
                for kc in range(4):
                    op("pe", lambda e, kc=kc, ba=ba, oc=oc: e.matmul(bank(ba, n), lhsT=w_ba[:, kc, oc * 128:(oc + 1) * 128],
                                                              rhs=goT[hi][:, kc, c0:c0 + n], start=(kc == 0), stop=(kc == 3)),
                       reads=[b_w["ba"]] + b_goT[hi], writes=[b_bank[ba]])
                op("dve", lambda e, bp=bp: e.scalar_tensor_tensor(out=sg0[:, 0:n], in0=sg0[:, 0:n], scalar=1.0,
                                                                  in1=bank(bp, n), op0=ALU.add, op1=ALU.mult),
                   reads=[b_sg0, b_bank[bp]], writes=[b_sg0])
                rel(bp)
                op("dve", lambda e, ba=ba: e.scalar_tensor_tensor(out=sg1[:, 0:n], in0=sg1[:, 0:n], scalar=1.0,
                                                                  in1=bank(ba, n), op0=ALU.add, op1=ALU.mult),
                   reads=[b_sg1, b_bank[ba]], writes=[b_sg1])
                rel(ba)
                op("pool", lambda e, oc=oc: e.tensor_tensor(out=mT[:, oc, c0:c0 + n], in0=sg0[:, 0:n], in1=sg1[:, 0:n],
                                                            op=ALU.add),
                   reads=[b_sg0, b_sg1], writes=mbufs(oc))
                yield
            if s == 0:
                tap("mT0", mT[:, :, :], b_mT)

        b_ro_x = [[B(f"rox{i}_{k}") for k in range(2)] for i in range(2)]

        def ro_bufs(s):
            lst = [(ro[0][:, :], b_ro[0], []), (ro[1][:, :], b_ro[1], [])]
            if s >= NS - 2:
                hx = (NS - 2) % 2
                ext = hT[hx][:, :, :].rearrange("p k w -> p (k w)").bitcast(F32)
                for k in range(2):
                    lst.append((ext[:, k * 1024:(k + 1) * 1024], b_ro_x[0][k], b_hT[hx]))
            return lst

        def o_preload(s):
            bufs = ro_bufs(s)
            for t in range(4):
                rap, brb, extra = bufs[t]
                load(rap, x_rows(s, t), [brb] + extra)

        def stage_o(s, tiles=range(4), split=False, preloaded=False):
            junk, bjunk = (xs[0][:, :], b_xs[0]) if split else (sg0[:, :].bitcast(BF16), b_sg0)
            pend = None
            bufs = ro_bufs(s)
            four = len(bufs) == 4

            def epilogue(rap, brb, i, t):
                op("dve", lambda e: e.scalar_tensor_tensor(out=rap, in0=rap, scalar=st[:, i:i + 1],
                                                           in1=fg[:, :], op0=ALU.mult, op1=ALU.mult),
                   reads=[brb, b_st[i], b_fg], writes=[brb])
                bo = B(f"out{s}_{t}")
                b_out.append(bo)
                r0 = s * W + t * 128
                op("pool", lambda e: e.dma_start(out=out_c[r0:r0 + 128, :], in_=rap),
                   reads=[brb], writes=[bo], dma=True)

            if four and not preloaded:
                for t in tiles:
                    rap, brb, extra = bufs[t]
                    load(rap, x_rows(s, t), [brb] + extra)
            yield
            for t in tiles:
                if four:
                    rap, brb, _ = bufs[t]
                else:
                    ri = cnt["ro"] % 2
                    cnt["ro"] += 1
                    rap, brb, _ = bufs[ri]
                    load(rap, x_rows(s, t), [brb])
                yb = alloc2()
                for half in range(2):
                    for kc in range(8):
                        op("pe", lambda e, kc=kc, half=half, yb=yb, t=t: e.matmul(
                            bank(yb + half), lhsT=mT[:, kc, t * 128:(t + 1) * 128],
                            rhs=w_out[:, kc, half * 512:(half + 1) * 512], start=(kc == 0), stop=(kc == 7)),
                           reads=[b_w["out"]] + (b_mTh[t // 2] if split else b_mT), writes=[b_bank[yb + half]])
                    if half == 0:
                        if pend is not None:
                            epilogue(*pend)
                            pend = None
                        yield
                Y = ps[:, yb * 512:(yb + 2) * 512]
                op("dve", lambda e, Y=Y, rap=rap: e.scalar_tensor_tensor(out=rap, in0=Y, scalar=0.5,
                                                                         in1=rap, op0=ALU.mult, op1=ALU.add),
                   reads=[b_bank[yb], b_bank[yb + 1], brb], writes=[brb])
                rel(yb, yb + 1)
                i = rstd_of(rap, [brb], junk, [bjunk])
                pend = (rap, brb, i, t)
                yield
            epilogue(*pend)
            yield

        def drive(gens, gnext=None, on_qfree=None):
            gens = [(g, 1) if not isinstance(g, tuple) else g for g in gens if g is not None]
            while gens:
                for ent in list(gens):
                    g, k = ent
                    for _ in range(k):
                        try:
                            r = next(g)
                            if r == "pf" and gnext is not None:
                                next(gnext)
                            if r == "qfree" and on_qfree is not None:
                                gens.append((on_qfree, 1))
                        except StopIteration:
                            gens.remove(ent)
                            break

        weights_mid()
        ga1 = stage_a1(1)
        drive([chain(stage_a1(0), stage_a2(0))], ga1)
        weights_late()
        drive([stage_b(0), ga1], on_qfree=stage_a2(1))
        tap("hT0", hT[0][:, :, :], b_hT[0])
        tap("goT0", goT[0][:, :, :], b_goT[0])
        tap("gpT0", gpT[0][:, :, :], b_gpT[0])
        for s in range(NS - 1):
            if s + 2 < NS:
                ga = stage_a1(s + 2)
                drive([stage_b(s + 1), stage_m(s)], ga)
                drive([chain(ga, stage_a2(s + 2)), stage_o(s)])
            else:
                drive([stage_b(s + 1), (chain(stage_m(s), stage_o(s)), 2)])
        o_preload(NS - 1)
        drive([stage_m(NS - 1, half=0)])
        drive([stage_m(NS - 1, half=1), stage_o(NS - 1, tiles=range(0, 2), split=True, preloaded=True)])
        drive([stage_o(NS - 1, tiles=range(2, 4), split=True, preloaded=True)])

        op("sp", None, reads=b_out)
        op("pool", None, reads=b_out)

        keys = set()
        for stream in S.streams.values():
            for fn, wl, key, inc in stream:
                if key is not None:
                    keys.add(key)
        sems = {k: es.enter_context(nc.semaphore("s_" + (k if isinstance(k, str) else f"{k[0]}{k[1]}")))
                for k in sorted(keys, key=str)}

        def replay(eng, name):
            for fn, wl, key, inc in S.streams[name]:
                for k, v in wl:
                    eng.wait_ge(sems[k], v)
                if fn is not None:
                    fn(eng).then_inc(sems[key], inc)

        with nc.Block() as block:
            @block.tensor
            def _(e):
                replay(e, "pe")

            @block.scalar
            def _(e):
                replay(e, "act")

            @block.vector
            def _(e):
                replay(e, "dve")

            @block.gpsimd
            def _(e):
                replay(e, "pool")

            @block.sync
            def _(e):
                replay(e, "sp")
    return nc


def _host_layout(x, norm_gain, w_in, pool_w, pool_scale, attn_sinks, w_branch_pool, w_branch_attn, w_out, final_gain):
    f32 = np.float32
    perm = np.concatenate([
        1536 + np.arange(128), 1664 + np.arange(128), np.arange(512),
        np.concatenate([np.concatenate([1024 + c * 64 + np.arange(64), 1024 + (4 + c) * 64 + np.arange(64)])
                        for c in range(4)]),
        1792 + np.arange(512), 512 + np.arange(512), 2304 + np.arange(2048)])
    kmaj = lambda w, kc: np.ascontiguousarray(w.reshape(kc, 128, w.shape[1]).transpose(1, 0, 2))
    shared = {
        "w_in_l": kmaj(np.asarray(w_in[0], f32)[:, perm], 8),
        "w_bp_l": kmaj(np.asarray(w_branch_pool[0], f32), 4),
        "w_ba_l": kmaj(np.asarray(w_branch_attn[0], f32), 4),
        "w_out_l": kmaj(np.asarray(w_out[0], f32), 8),
        "pool_w_l": np.ascontiguousarray(np.asarray(pool_w[0], f32).transpose(1, 0, 2)),
        "fg_b": np.ascontiguousarray(np.broadcast_to(np.asarray(final_gain, f32)[None, :], (128, D))),
    }
    idr = np.zeros((128, 256), f32)
    idr[:, 0:128] = np.eye(128, dtype=f32)
    for a in range(128):
        idr[a, 128 + (a + 32 if a % 64 < 32 else a - 32)] = 1.0
    shared["idr"] = idr
    sidx, tidx = np.meshgrid(np.arange(128), np.arange(128), indexing="ij")
    mp = (tidx < sidx).astype(f32)
    mc = (tidx >= sidx).astype(f32)
    inv_freq = 10000.0 ** (-(np.arange(0, 64, 2, dtype=np.float64) / 64.0))
    x = np.asarray(x, f32)
    maps = []
    for c in range(NCORES):
        b, j = divmod(c, 4)
        p0 = j * T
        m = dict(shared)
        m["x_c"] = np.ascontiguousarray(x[b, p0:p0 + T])
        m["x_h"] = np.ascontiguousarray(x[b, p0 - 128:p0]) if p0 > 0 else np.zeros((128, D), f32)
        pos = np.arange(p0 - 128, p0 + T, dtype=np.float64)
        ang = (pos.astype(f32)[None, :] * inv_freq.astype(f32)[:, None]).astype(f32).astype(np.float64)
        cosr, sinr = np.cos(ang).astype(f32), np.sin(ang).astype(f32)
        m["cosT"] = np.ascontiguousarray(np.tile(cosr, (4, 1)))
        m["sinT"] = np.ascontiguousarray(np.concatenate([-sinr, sinr, -sinr, sinr], axis=0))
        m["masks"] = np.ascontiguousarray(np.concatenate([mp, mc, mp if p0 > 0 else np.zeros_like(mp), mc], axis=1))
        cst = np.zeros((128, NCST), f32)
        cst[:, CS_GAIN:CS_GAIN + 8] = np.asarray(norm_gain[0], f32).reshape(8, 128).T
        cst[:, CS_PSC:CS_PSC + 4] = np.asarray(pool_scale[0], f32).reshape(4, 128).T
        cst[:, CS_SINK:CS_SINK + 8] = np.asarray(attn_sinks[0], f32)[None, :]
        for g in range(4):
            wdw = 2 << g
            cntv = np.minimum(np.arange(p0, p0 + 16) + 1, wdw).astype(f32)
            cst[:, CS_INVC + g * 16:CS_INVC + (g + 1) * 16] = (1.0 / cntv)[None, :]
        cst[:, CS_NEGH] = -0.5
        m["cst"] = cst
        maps.append(m)
    return maps


_NC_CACHE = {}


def kernel(x, norm_gain, w_in, pool_w, pool_scale, attn_sinks, w_branch_pool, w_branch_attn, w_out, final_gain):
    maps = _host_layout(x, norm_gain, w_in, pool_w, pool_scale, attn_sinks, w_branch_pool, w_branch_attn, w_out,
                        final_gain)
    if "nc" not in _NC_CACHE:
        _NC_CACHE["nc"] = build_program()
    res = run_bass_kernel_spmd(_NC_CACHE["nc"], maps, core_ids=list(range(NCORES)))
    out = np.empty((2, SEQ, D), np.float32)
    for c in range(NCORES):
        b, j = divmod(c, 4)
        out[b, j * T:(j + 1) * T] = res.results[c]["out_c"]
    return out
```
